# Optimizing a Trainium2 kernel written in Bass

```python
import math
import jax
import jax.numpy as jnp
from jax import lax
import numpy as np

D_MODEL = 1024
BATCH = 4
SEQ = 8192
DEPTH = 4

MIX_WIDTH = D_MODEL
SSD_HEADS = 8
SSD_HEAD_DIM = 64
SSD_INNER = SSD_HEADS * SSD_HEAD_DIM
SSD_GROUPS = 2
SSD_STATE = 128
SSD_CONV = 4
SSD_CHUNK = 256
SSD_XBC = SSD_INNER + 2 * SSD_GROUPS * SSD_STATE
SC_HEADS = 4
SC_HEAD_DIM = 64
SC_WIDTH = SC_HEADS * SC_HEAD_DIM
SC_CONV = 3
NSA_HEADS = 4
NSA_HEAD_DIM = 64
NSA_WIDTH = NSA_HEADS * NSA_HEAD_DIM
NSA_KV = NSA_HEAD_DIM
CMP_BLOCK = 32
CMP_STRIDE = 16
CMP_HIDDEN = 128
SEL_BLOCK = 64
SEL_TOPK = 16
SEL_LOCAL = 2
WINDOW = 512
Q_BLOCK = 128
ROPE_THETA = 10000.0
D_FF = 2752
FFN_CONV = 3
EPS = 1e-6
NEG = -1e30
FORCE = 1e9
IN_SIZES = (SSD_INNER, SSD_XBC, SSD_HEADS, SC_WIDTH, SC_WIDTH, SC_WIDTH, NSA_WIDTH,
            NSA_KV, NSA_KV, NSA_KV, NSA_KV, NSA_KV, NSA_KV, 3 * NSA_HEADS)
N_IN = sum(IN_SIZES)

kernel_name = 'hymba_ssd_shortconv_nsa_trunk'


def rmsnorm(x, w):
    xf = x.astype(jnp.float32)
    y = xf * lax.rsqrt(jnp.mean(xf * xf, axis=-1, keepdims=True) + EPS)
    return (y * w.astype(jnp.float32)).astype(x.dtype)


def causal_dwconv(u, w, b=None):
    width = w.shape[0]
    s_len = u.shape[1]
    up = jnp.pad(u, ((0, 0), (width - 1, 0), (0, 0)))
    out = up[:, 0:s_len] * w[0]
    for j in range(1, width):
        out = out + up[:, j:j + s_len] * w[j]
    if b is not None:
        out = out + b
    return out


def rope_tables(s_len):
    half = NSA_HEAD_DIM // 2
    inv = 1.0 / (ROPE_THETA ** (jnp.arange(half, dtype=jnp.float32) / half))
    ang = jnp.arange(s_len, dtype=jnp.float32)[:, None] * inv[None, :]
    return jnp.cos(ang), jnp.sin(ang)


def apply_rope(x, cos, sin):
    x1, x2 = jnp.split(x, 2, axis=-1)
    c = cos[None, :, None, :].astype(x.dtype)
    s = sin[None, :, None, :].astype(x.dtype)
    return jnp.concatenate([x1 * c - x2 * s, x2 * c + x1 * s], axis=-1)


def masked_softmax(s, mask):
    s = jnp.where(mask, s.astype(jnp.float32), NEG)
    m = jnp.max(s, axis=-1, keepdims=True)
    p = jnp.exp(s - m) * mask
    return p / jnp.maximum(jnp.sum(p, axis=-1, keepdims=True), 1e-20)


def ssd_mixer(z, xbc, dt_raw, conv_w, conv_b, dt_bias, a_log, d_skip, norm_w):
    f32 = jnp.float32
    bsz, s_len, _ = z.shape
    n_k = SSD_HEADS // SSD_GROUPS
    xbc = jax.nn.silu(causal_dwconv(xbc, conv_w, conv_b))
    xs, b_in, c_in = jnp.split(xbc, [SSD_INNER, SSD_INNER + SSD_GROUPS * SSD_STATE], axis=-1)
    dt = jax.nn.softplus((dt_raw + dt_bias).astype(f32))
    a_neg = -jnp.exp(a_log.astype(f32)).reshape(SSD_GROUPS, n_k)
    pad = (-s_len) % SSD_CHUNK
    s_pad = s_len + pad
    n_c = s_pad // SSD_CHUNK

    def chunked(t, *tail):
        t = jnp.pad(t.astype(f32), ((0, 0), (0, pad), (0, 0)))
        return t.reshape(bsz, n_c, SSD_CHUNK, *tail)

    x_c = chunked(xs, SSD_GROUPS, n_k, SSD_HEAD_DIM)
    dt_c = chunked(dt, SSD_GROUPS, n_k)
    b_c = chunked(b_in, SSD_GROUPS, SSD_STATE)
    c_c = chunked(c_in, SSD_GROUPS, SSD_STATE)
    xdt = x_c * dt_c[..., None]
    a = jnp.transpose(dt_c * a_neg, (0, 3, 4, 1, 2))
    a_cs = jnp.cumsum(a, axis=-1)
    causal = jnp.tril(jnp.ones((SSD_CHUNK, SSD_CHUNK), dtype=bool))
    seg = a_cs[..., :, None] - a_cs[..., None, :]
    decay = jnp.exp(jnp.where(causal, seg, -jnp.inf))
    cb = jnp.einsum('bclgn,bcsgn->bgcls', c_c, b_c)
    y_diag = jnp.einsum('bgkcls,bcsgkp->bclgkp', decay * cb[:, :, None], xdt)
    decay_to_end = jnp.exp(a_cs[..., -1:] - a_cs)
    chunk_states = jnp.einsum('bclgn,bgkcl,bclgkp->cbgkpn', b_c, decay_to_end, xdt)
    chunk_decay = jnp.moveaxis(jnp.exp(a_cs[..., -1]), -1, 0)

    def step(h, inp):
        s_new, d_c = inp
        return h * d_c[..., None, None] + s_new, h

    h0 = jnp.zeros((bsz, SSD_GROUPS, n_k, SSD_HEAD_DIM, SSD_STATE), f32)
    _, prev_states = lax.scan(step, h0, (chunk_states, chunk_decay))
    y_off = jnp.einsum('bclgn,cbgkpn,bgkcl->bclgkp', c_c, prev_states, jnp.exp(a_cs))
    y = y_diag + y_off + x_c * d_skip.astype(f32).reshape(SSD_GROUPS, n_k)[:, :, None]
    y = y.reshape(bsz, s_pad, SSD_INNER)[:, :s_len]
    y = y * jax.nn.silu(z.astype(f32))
    return rmsnorm(y, norm_w).astype(z.dtype)


def compress_block(t, pos, w1, w2):
    bsz, s_len, dk = t.shape
    n_cmp = (s_len - CMP_BLOCK) // CMP_STRIDE + 1
    idx = np.arange(n_cmp)[:, None] * CMP_STRIDE + np.arange(CMP_BLOCK)[None, :]
    blocks = (t[:, idx] + pos).reshape(bsz, n_cmp, CMP_BLOCK * dk)
    return jax.nn.gelu(blocks @ w1) @ w2


def nsa_mixer(q, k_cmp, v_cmp, k_sel, v_sel, k_win, v_win, gate_logits, cos, sin,
              kpos, kw1, kw2, vpos, vw1, vw2):
    bsz, s_len, _ = q.shape
    dk = NSA_HEAD_DIM
    q = apply_rope(q.reshape(bsz, s_len, NSA_HEADS, dk), cos, sin) * (dk ** -0.5)

    def rope_k(t):
        return apply_rope(t[:, :, None, :], cos, sin)[:, :, 0, :]

    kc = compress_block(rope_k(k_cmp), kpos, kw1, kw2)
    vc = compress_block(v_cmp, vpos, vw1, vw2)
    n_cmp = kc.shape[1]
    n_slc = s_len // SEL_BLOCK
    top = min(SEL_TOPK, n_slc)
    cmp_start = np.arange(n_cmp) * CMP_STRIDE
    slc_start = np.arange(n_slc) * SEL_BLOCK
    cmp_end = jnp.asarray(cmp_start + CMP_BLOCK - 1, jnp.int32)
    overlap = jnp.asarray(((cmp_start[:, None] < slc_start[None, :] + SEL_BLOCK)
                           & (cmp_start[:, None] + CMP_BLOCK > slc_start[None, :])).astype(np.float32))
    ks_blocks = rope_k(k_sel).reshape(bsz, n_slc, SEL_BLOCK, dk)
    vs_blocks = v_sel.reshape(bsz, n_slc, SEL_BLOCK, dk)
    win_pad = ((0, 0), (WINDOW, 0), (0, 0))
    kw = jnp.pad(rope_k(k_win), win_pad)
    vw = jnp.pad(v_win, win_pad)
    gates = jax.nn.sigmoid(gate_logits.astype(jnp.float32)).reshape(bsz, s_len, NSA_HEADS, 3)
    n_qb = s_len // Q_BLOCK

    def to_blocks(t):
        return jnp.moveaxis(t.reshape(bsz, n_qb, Q_BLOCK, *t.shape[2:]), 1, 0)

    blk = jnp.arange(n_slc)
    offs = jnp.arange(SEL_BLOCK)
    gather = jax.vmap(lambda blocks, idx: blocks[idx])

    def block_fn(args):
        qb, gb, qi = args
        t = qi * Q_BLOCK + jnp.arange(Q_BLOCK)
        p_c = masked_softmax(jnp.einsum('bqhd,bnd->bhqn', qb, kc), cmp_end[None, :] <= t[:, None])
        o_c = jnp.einsum('bhqn,bnd->bqhd', p_c, vc)
        imp = jnp.einsum('bhqn,nj->bqj', p_c, overlap)
        cur = (t // SEL_BLOCK)[:, None]
        valid = blk[None, :] <= cur
        forced = (blk[None, :] == 0) | (valid & (blk[None, :] > cur - SEL_LOCAL))
        imp = jnp.where(forced, FORCE, jnp.where(valid, imp, NEG))
        _, sel = lax.top_k(imp, top)
        ks = gather(ks_blocks, sel).reshape(bsz, Q_BLOCK, top * SEL_BLOCK, dk)
        vs = gather(vs_blocks, sel).reshape(bsz, Q_BLOCK, top * SEL_BLOCK, dk)
        tok = (sel[..., None] * SEL_BLOCK + offs).reshape(bsz, Q_BLOCK, top * SEL_BLOCK)
        m_s = (tok <= t[None, :, None])[:, None]
        p_s = masked_softmax(jnp.einsum('bqhd,bqnd->bhqn', qb, ks), m_s)
        o_s = jnp.einsum('bhqn,bqnd->bqhd', p_s, vs)
        start = qi * Q_BLOCK
        kwb = lax.dynamic_slice_in_dim(kw, start, Q_BLOCK + WINDOW, axis=1)
        vwb = lax.dynamic_slice_in_dim(vw, start, Q_BLOCK + WINDOW, axis=1)
        kp = start - WINDOW + jnp.arange(Q_BLOCK + WINDOW)
        m_w = (kp[None, :] >= 0) & (kp[None, :] <= t[:, None]) & (kp[None, :] > t[:, None] - WINDOW)
        p_w = masked_softmax(jnp.einsum('bqhd,bkd->bhqk', qb, kwb), m_w)
        o_w = jnp.einsum('bhqk,bkd->bqhd', p_w, vwb)
        return o_c * gb[..., 0:1] + o_s * gb[..., 1:2] + o_w * gb[..., 2:3]

    out = lax.map(block_fn, (to_blocks(q), to_blocks(gates), jnp.arange(n_qb)))
    return jnp.moveaxis(out, 0, 1).reshape(bsz, s_len, NSA_WIDTH).astype(q.dtype)


def setup_inputs(seed: int = 0) -> dict:
    key = jax.random.key(seed)
    ks = jax.random.split(key, 24)
    f32 = jnp.float32
    nl = DEPTH
    cmp_in = CMP_BLOCK * NSA_HEAD_DIM

    def nrm(k, shape, scale):
        return jax.random.normal(k, shape, f32) * scale

    dt0 = jnp.exp(jax.random.uniform(ks[5], (nl, SSD_HEADS), f32, math.log(1e-3), math.log(1e-1)))
    return {
        'x': nrm(ks[0], (BATCH, SEQ, D_MODEL), 1.0),
        'attn_norm_w': 1.0 + nrm(ks[1], (nl, D_MODEL), 0.02),
        'w_in': nrm(ks[2], (nl, D_MODEL, N_IN), D_MODEL ** -0.5),
        'ssd_conv_w': nrm(ks[3], (nl, SSD_CONV, SSD_XBC), SSD_CONV ** -0.5),
        'ssd_conv_b': nrm(ks[4], (nl, SSD_XBC), 0.02),
        'ssd_dt_bias': dt0 + jnp.log(-jnp.expm1(-dt0)),
        'ssd_a_log': jnp.log(jax.random.uniform(ks[6], (nl, SSD_HEADS), f32, 1.0, 16.0)),
        'ssd_d': 1.0 + nrm(ks[7], (nl, SSD_HEADS), 0.02),
        'ssd_norm_w': 1.0 + nrm(ks[8], (nl, SSD_INNER), 0.02),
        'sc_conv_w': nrm(ks[9], (nl, SC_CONV, SC_WIDTH), SC_CONV ** -0.5),
        'cmp_k_pos': nrm(ks[10], (nl, CMP_BLOCK, NSA_HEAD_DIM), 0.02),
        'cmp_k_w1': nrm(ks[11], (nl, cmp_in, CMP_HIDDEN), cmp_in ** -0.5),
        'cmp_k_w2': nrm(ks[12], (nl, CMP_HIDDEN, NSA_HEAD_DIM), CMP_HIDDEN ** -0.5),
        'cmp_v_pos': nrm(ks[13], (nl, CMP_BLOCK, NSA_HEAD_DIM), 0.02),
        'cmp_v_w1': nrm(ks[14], (nl, cmp_in, CMP_HIDDEN), cmp_in ** -0.5),
        'cmp_v_w2': nrm(ks[15], (nl, CMP_HIDDEN, NSA_HEAD_DIM), CMP_HIDDEN ** -0.5),
        'w_out': nrm(ks[16], (nl, MIX_WIDTH, D_MODEL), MIX_WIDTH ** -0.5),
        'ffn_norm_w': 1.0 + nrm(ks[17], (nl, D_MODEL), 0.02),
        'ffn_w_up': nrm(ks[18], (nl, D_MODEL, 2 * D_FF), D_MODEL ** -0.5),
        'ffn_conv_w': nrm(ks[19], (nl, FFN_CONV, 2 * D_FF), FFN_CONV ** -0.5),
        'ffn_conv_b': nrm(ks[20], (nl, 2 * D_FF), 0.02),
        'ffn_w_down': nrm(ks[21], (nl, D_FF, D_MODEL), D_FF ** -0.5),
        'final_norm_w': 1.0 + nrm(ks[22], (D_MODEL,), 0.02),
    }


def reference(x, attn_norm_w, w_in, ssd_conv_w, ssd_conv_b, ssd_dt_bias, ssd_a_log, ssd_d,
              ssd_norm_w, sc_conv_w, cmp_k_pos, cmp_k_w1, cmp_k_w2, cmp_v_pos, cmp_v_w1, cmp_v_w2,
              w_out, ffn_norm_w, ffn_w_up, ffn_conv_w, ffn_conv_b, ffn_w_down, final_norm_w):
    s_len = x.shape[1]
    cos, sin = rope_tables(s_len)
    split_at = np.cumsum(np.array(IN_SIZES))[:-1].tolist()
    for l in range(DEPTH):
        h = rmsnorm(x, attn_norm_w[l])
        proj = h @ w_in[l]
        (z, xbc, dt_raw, sc_b, sc_c, sc_h, q, k_c, v_c, k_s, v_s, k_w, v_w,
         g_nsa) = jnp.split(proj, split_at, axis=-1)
        y_ssd = ssd_mixer(z, xbc, dt_raw, ssd_conv_w[l], ssd_conv_b[l], ssd_dt_bias[l],
                          ssd_a_log[l], ssd_d[l], ssd_norm_w[l])
        y_sc = sc_b * causal_dwconv(sc_c * sc_h, sc_conv_w[l])
        y_nsa = nsa_mixer(q, k_c, v_c, k_s, v_s, k_w, v_w, g_nsa, cos, sin,
                          cmp_k_pos[l], cmp_k_w1[l], cmp_k_w2[l],
                          cmp_v_pos[l], cmp_v_w1[l], cmp_v_w2[l])
        x = x + jnp.concatenate([y_ssd, y_sc, y_nsa], axis=-1) @ w_out[l]
        h = rmsnorm(x, ffn_norm_w[l])
        u = causal_dwconv(h @ ffn_w_up[l], ffn_conv_w[l], ffn_conv_b[l])
        gate, val = jnp.split(u, 2, axis=-1)
        x = x + (jax.nn.silu(gate) * val) @ ffn_w_down[l]
    return rmsnorm(x, final_norm_w)
```

```python
import numpy as np
from contextlib import ExitStack
import concourse.bass as bass
import concourse.mybir as mybir
from concourse.bass_utils import run_bass_kernel_spmd

F32 = mybir.dt.float32
BF16 = mybir.dt.bfloat16
AF = mybir.ActivationFunctionType
ALU = mybir.AluOpType
AX = mybir.AxisListType

BIG = 30000.0
import os
SKIP96 = bool(int(os.environ.get('SKIP96', '0')))
T = 512
NCOL = 3732
D_FF = 2752
EPS = 1e-6


class Prog:
    ENG = ("pe", "dve", "act", "pool", "sp")
    NDMA = 12
    EPOCH = 16000

    def __init__(self, nc, stack):
        self.nc = nc
        self.stack = stack
        self.streams = {e: [] for e in self.ENG}
        self.count = {e: 0 for e in self.ENG}
        self.sems = {}
        self.waited = {e: {} for e in self.ENG}
        self.last_w = {}
        self.readers = {}
        self.ndma = 0
        self.dma_tok = {}

    def _deps(self, reads, writes):
        deps = {}

        def add(s, v):
            if deps.get(s, 0) < v:
                deps[s] = v
        for k in reads:
            t = self.last_w.get(k)
            if t is not None:
                add(*t)
        for k in writes:
            t = self.last_w.get(k)
            if t is not None:
                add(*t)
            for s, v in self.readers.get(k, {}).items():
                add(s, v)
        return deps

    def _commit(self, tok, reads, writes):
        s, v = tok
        for k in reads:
            r = self.readers.setdefault(k, {})
            if r.get(s, 0) < v:
                r[s] = v
        for k in writes:
            self.last_w[k] = tok
            self.readers[k] = {}

    def _waits(self, eng, deps):
        waits = []
        for s, v in deps.items():
            if eng == "pe" and s[0] == "pe":
                continue
            if self.waited[eng].get(s, 0) < v:
                self.waited[eng][s] = v
                waits.append((s, v))
        return waits

    def op(self, eng, fn, reads=(), writes=()):
        deps = self._deps(reads, writes)
        self.count[eng] += 1
        c = self.count[eng]
        sem = (eng, (c - 1) // self.EPOCH)
        val = (c - 1) % self.EPOCH + 1
        self.streams[eng].append((self._waits(eng, deps), fn, sem, val, 1))
        self._commit((sem, val), reads, writes)

    def dma(self, out, in_, reads=(), writes=(), eng="sp"):
        deps = self._deps(reads, writes)
        j = self.ndma % self.NDMA
        r = self.ndma // self.NDMA
        self.ndma += 1
        sem = ("dma", j)
        if r > 0:
            deps[sem] = max(deps.get(sem, 0), 16 * r)
        fn = lambda e: e.dma_start(out=out, in_=in_)
        self.streams[eng].append((self._waits(eng, deps), fn, sem, 16 * (r + 1), 16))
        self._commit((sem, 16 * (r + 1)), reads, writes)

    def barrier(self):
        toks = {}
        for e in self.ENG:
            c = self.count[e]
            if c > 0:
                toks[(e, (c - 1) // self.EPOCH)] = (c - 1) % self.EPOCH + 1
        for j in range(min(self.ndma, self.NDMA)):
            n = (self.ndma - 1 - j) // self.NDMA + 1
            toks[("dma", j)] = 16 * n
        for e in self.ENG:
            w = self._waits(e, dict(toks))
            if w:
                self.streams[e].append((w, None, None, 0, 0))

    def emit(self, final_keys=()):
        nc = self.nc
        fw = list(self._deps(final_keys, ()).items())
        names = set()
        for e in self.ENG:
            for (waits, fn, sem, val, inc) in self.streams[e]:
                if sem is not None:
                    names.add(sem)
                for s, v in waits:
                    names.add(s)
        for s, v in fw:
            names.add(s)
        for s in sorted(names, key=str):
            self.sems[s] = self.stack.enter_context(nc.semaphore("s_" + "_".join(str(x) for x in s)))
        engobj = {"pe": "tensor", "dve": "vector", "act": "scalar", "pool": "gpsimd", "sp": "sync"}
        with nc.Block() as block:
            for e in self.ENG:
                stream = self.streams[e]
                if not stream and e != "sp":
                    continue

                def body(eo, stream=stream, e=e):
                    for (waits, fn, sem, val, inc) in stream:
                        for s, v in waits:
                            eo.wait_ge(self.sems[s], v)
                        if fn is not None:
                            fn(eo).then_inc(self.sems[sem], inc)
                    if e == "sp":
                        for s, v in fw:
                            eo.wait_ge(self.sems[s], v)
                getattr(block, engobj[e])(body)


class Arena:
    def __init__(self, buf, n):
        self.buf = buf
        self.n = n
        self.off = 0

    def bf(self, cols):
        a = self.buf[:, self.off:self.off + cols]
        self.off += (cols + 15) // 16 * 16
        assert self.off <= self.n, ("arena overflow", self.off, self.n)
        return a

    def f32(self, cols):
        a = self.buf[:, self.off:self.off + 2 * cols].bitcast(F32)
        self.off += (2 * cols + 15) // 16 * 16
        assert self.off <= self.n, ("arena overflow", self.off, self.n)
        return a


def build(S, NL, taps=None, NAR_=106400):
    NT = S // T
    nc = bass.Bass("TRN2", target_bir_lowering=False)

    def Din(name, shape):
        return nc.dram_tensor(name, shape, F32, kind="ExternalInput").ap()
    xT_d = Din("xT", [128, 8, S])
    win_d = Din("w_in", [NL, 128, 8, NCOL])
    wout_d = Din("w_out", [NL, 128, 8, 1024])
    wup_d = Din("w_up", [NL, 128, 8, 5504])
    wdn_d = Din("w_dn", [NL, D_FF, 1024])
    pv_d = Din("pv", [NL, 128, 238])
    rowv_d = Din("rowv", [NL, 1, 536])
    w1r_d = Din("w1r", [NL, 128, 32, 128])
    posT_d = Din("posT", [NL, 128, 32])
    w2k_d = Din("w2k", [NL, 128, 128])
    w2v_d = Din("w2v", [NL, 128, 64])
    fnw_d = Din("fnw", [128, 8])
    rope_d = Din("rope", [4, 128, S])
    cstb_d = Din("cstb", [128, 4480])
    cstf_d = Din("cstf", [128, 523])
    outT_d = nc.dram_tensor("outT", [128, 8, S], F32, kind="ExternalOutput").ap()
    X1 = nc.dram_tensor("X1", [128, 8, S], F32).ap()
    X2 = nc.dram_tensor("X2", [128, 8, S], F32).ap()
    tap_d = {}
    if taps:
        for name, shape in taps.items():
            tap_d[name] = nc.dram_tensor("tap_" + name, shape, F32, kind="ExternalOutput").ap()

    with ExitStack() as st:
        P = Prog(nc, st)
        NAR = NAR_
        arena_t = st.enter_context(nc.sbuf_tensor("arena", [128, NAR], BF16))
        A = Arena(arena_t, NAR)
        psb = [st.enter_context(nc.psum_tensor("ps%d" % i, [128, 512], F32)) for i in range(8)]
        psrot = [0]

        def psum():
            i = psrot[0] % 6
            psrot[0] += 1
            return psb[i], "ps%d" % i

        def mm(out, lhsT, rhs, start, stop, reads, writes):
            P.op("pe", lambda e: e.matmul(out, lhsT=lhsT, rhs=rhs, start=start, stop=stop), reads, writes)

        def mmg(out, lhsT, rhs, start, stop, reads, writes):
            P.op("pe", lambda e: e.matmul(out, lhsT=lhsT, rhs=rhs, start=start, stop=stop, skip_group_check=True), reads, writes)

        def tr(out, in_, ident, reads, writes):
            P.op("pe", lambda e: e.transpose(out, in_, ident), reads, writes)

        def act(out, in_, func, reads, writes, bias=None, scale=None, accum=None):
            kw = {}
            if bias is not None:
                kw["bias"] = bias
            if scale is not None:
                kw["scale"] = scale
            if accum is not None:
                kw["accum_out"] = accum
            P.op("act", lambda e: e.activation(out=out, in_=in_, func=func, **kw), reads, writes)

        def tt(eng, out, in0, in1, op, reads, writes):
            P.op(eng, lambda e: e.tensor_tensor(out=out, in0=in0, in1=in1, op=op), reads, writes)

        def ts(eng, out, in0, s1, s2, op0, op1, reads, writes):
            if s2 is None:
                P.op(eng, lambda e: e.tensor_scalar(out=out, in0=in0, scalar1=s1, scalar2=None, op0=op0), reads, writes)
            else:
                P.op(eng, lambda e: e.tensor_scalar(out=out, in0=in0, scalar1=s1, scalar2=s2, op0=op0, op1=op1), reads, writes)

        def stt(eng, out, in0, scalar, in1, op0, op1, reads, writes):
            eng = "dve"
            P.op(eng, lambda e: e.scalar_tensor_tensor(out=out, in0=in0, scalar=scalar, in1=in1, op0=op0, op1=op1), reads, writes)

        def cp(eng, out, in_, reads, writes):
            P.op(eng, lambda e: e.tensor_copy(out=out, in_=in_), reads, writes)

        def mset(eng, out, val, writes):
            P.op(eng, lambda e: e.memset(out, val), (), writes)

        def tap(name, src, key, idx=None):
            if name in tap_d:
                dst = tap_d[name] if idx is None else tap_d[name][idx]
                P.dma(dst, src, reads=[key], writes=["tap_" + name], eng="pool")

        cb = A.bf(4480)
        ident = cb[:, 0:128]
        MC = cb[:, 128:1024]
        MW = cb[:, 1024:2432]
        HW32 = cb[:, 2432:4480]
        cf = A.f32(523)
        U1 = cf[:, 0:256]
        UT = cf[:, 0:128]
        ONES = cf[:, 128:256]
        WS = cf[:, 256:384]
        PB = cf[:, 384:392]
        FPt = cf[:, 392:395]
        identf = cf[:, 395:523]
        pv = A.f32(238)
        fnw = A.f32(8)
        xT = A.f32(8 * T)
        base_off = A.off

        P.dma(cb, cstb_d[:, :], writes=["cb"], eng="pool")
        P.dma(cf, cstf_d[:, :], writes=["cf"])
        P.dma(fnw, fnw_d[:, :], writes=["fnw"])

        NRM = {}

        def rms_stats():
            ps, pk = psum()
            for c in range(8):
                sqb, sqk = NRM["sq"][c % 2]
                act(sqb, xT[:, c * T:(c + 1) * T], AF.Square, ["xT"], [sqk])
                mm(ps[:, :], ONES, sqb, c == 0, c == 7, [sqk, "cf"], [pk])
            rb, rk = NRM["rstd"]
            act(rb, ps[:, :], AF.Sqrt, [pk], [rk], bias=EPS_AP, scale=1.0 / 1024.0)
            P.op("dve", lambda e: e.reciprocal(out=rb, in_=rb), [rk], [rk])
            return rb, rk

        def rmsnorm_T(woff, wt=None, wkey="pv"):
            wt = pv if wt is None else wt
            rb, rk = rms_stats()
            for c in range(8):
                stt("dve", hT[:, c * T:(c + 1) * T], xT[:, c * T:(c + 1) * T],
                    wt[:, woff + c:woff + c + 1], rb, ALU.mult, ALU.mult, ["xT", rk, wkey], ["hT"])

        epsb = A.f32(1)
        mset("dve", epsb, EPS, ["epsb"])
        EPS_AP = epsb[:, 0:1]
        base_off = A.off

        for l in range(NL):
            P.barrier()
            A.off = base_off
            w_in = A.bf(8 * NCOL)
            w_out = A.bf(8 * 1024)
            w1r = A.bf(32 * 128)
            w2k = A.bf(128)
            w2v = A.bf(64)
            posT = A.bf(32)
            rowb = A.f32(536)
            aneg = A.f32(8)
            cbias = A.f32(2)
            K1r = A.bf(S)
            Vs = A.bf((S // 128) * 65)
            Kw = A.bf(2 * T)
            Vw = A.bf(8 * 65)
            K3 = A.bf(16 + T)
            kcT = A.bf(512)
            vcc = A.bf(4 * 64)
            xcar = A.f32(8 * 3)
            cccar = A.f32(2 * 2)
            HTs = A.f32(512)
            HTb = A.bf(512)
            xbc = A.bf(8 * T)
            QT = A.bf(2 * T)
            zs = A.bf(4 * 512)
            dtt = A.f32(4 * 8)
            att = A.f32(4 * 8)
            gat = A.f32(4 * 12)
            ymT = A.bf(8 * T)
            xst = A.bf(4 * 512)
            Btk = A.bf(4 * 256)
            R_off = A.off
            hT = A.bf(8 * T)
            ropeA = A.f32(2 * T)
            ropeB = A.f32(2 * T)
            stage = A.f32(3 + T)
            acc = A.f32(T)
            acc2 = A.f32(T)
            scb = A.bf(2 * T)
            ccs = A.f32(2 * (2 + T))
            NRM.update(sq=[(acc, "acc"), (acc2, "acc2")], rstd=(stage[:, 0:T], "stage"))
            R_end = A.off
            A.off = R_off
            xdt = A.bf(2 * 512)
            xdte = A.bf(2 * 512)
            xdtf = A.f32(512)
            acs = A.f32(16)
            tot = A.f32(8)
            eacs = A.f32(16)
            dte = A.f32(16)
            cdec = A.f32(8)
            aU0_ = [A.f32(256) for _ in range(2)]
            aU1_ = [A.f32(128) for _ in range(2)]
            CBT = A.f32(2 * 2 * 256)
            E0_ = [A.f32(256) for _ in range(2)]
            E1_ = [A.f32(128) for _ in range(2)]
            MT0_ = [A.bf(256) for _ in range(2)]
            MT1_ = [A.bf(128) for _ in range(2)]
            ysb = A.f32(512)
            ysb2 = A.f32(512)
            ynb = A.bf(512)
            sm1 = A.f32(4)
            R_end = max(R_end, A.off)
            A.off = R_off
            Pc = [A.f32(512) for _ in range(2)]
            Psum_ = A.f32(520)
            Pcb_ = [A.bf(512) for _ in range(2)]
            PcT = A.bf(4 * 128)
            impb = A.f32(128)
            Vb = A.f32(128)
            Vb2 = A.f32(128)
            m8 = A.f32(16)
            selb = A.bf(128)
            selbT = A.bf(T)
            PT = [A.bf(T) for _ in range(3)]
            yq = A.f32(4 * 256)
            rs = A.f32(8)
            ynsa = A.bf(256)
            hidb = A.bf(2 * 32)
            hidv = A.bf(128)
            HW96 = A.bf(2048)
            R_end = max(R_end, A.off)
            A.off = R_end
            print('phaseA arena', A.off, 'base', base_off)

            for c in range(8):
                P.dma(w_in[:, c * NCOL:(c + 1) * NCOL], win_d[l, :, c, :], writes=["w_in"], eng="pool")
            for c in range(8):
                P.dma(w_out[:, c * 1024:(c + 1) * 1024], wout_d[l, :, c, :], writes=["w_out"], eng="pool")
            for j4 in range(4):
                P.dma(w1r[:, j4 * 1024:(j4 + 1) * 1024].rearrange("p (j h) -> p j h", h=128),
                      w1r_d[l, :, j4 * 8:(j4 + 1) * 8, :], writes=["w1r"], eng="pool")
            P.dma(w2k, w2k_d[l], writes=["w2k"], eng="pool")
            P.dma(w2v, w2v_d[l], writes=["w2v"], eng="pool")
            P.dma(posT, posT_d[l], writes=["posT"], eng="pool")
            P.dma(pv, pv_d[l], writes=["pv"])
            P.dma(rowb, rowv_d[l].partition_broadcast(128), writes=["rowb"])
            dtb = rowb[:, 0:8]
            Dsk = rowb[:, 16:24]
            nrmw = rowb[:, 24:536]
            act(aneg, rowb[:, 8:16], AF.Exp, ["rowb"], ["aneg"])
            ts("dve", aneg, aneg, -1.0, None, ALU.mult, None, ["aneg"], ["aneg"])
            for half in range(2):
                ps, pk = psum()
                r0 = half * 64
                for j in range(32):
                    mm(ps[:, 0:1], w1r[r0:r0 + 64, j * 128:(j + 1) * 128], posT[r0:r0 + 64, j:j + 1],
                       j == 0, j == 31, ["w1r", "posT"], [pk])
                cp("dve", cbias[:, half:half + 1], ps[:, 0:1], [pk], ["cbias"])
            mset("dve", HTs, 0.0, ["HTs"])
            mset("pool", HTb, 0.0, ["HTb"])
            mset("dve", xcar, 0.0, ["xcar"])
            mset("dve", cccar, 0.0, ["cccar"])
            mset("pool", K3, 0.0, ["K3"])
            mset("pool", Vs, 1.0, ["Vs"])
            mset("pool", Vw, 1.0, ["Vw"])
            mset("pool", kcT, 0.0, ["kcT"])
            mset("pool", vcc, 0.0, ["vcc"])
            mset("pool", Kw, 0.0, ["Kw"])

            src = xT_d if l == 0 else X1
            for ti in range(NT):
                t0 = ti * T
                for c in range(8):
                    P.dma(xT[:, c * T:(c + 1) * T], src[:, c, t0:t0 + T], writes=["xT"])
                P.dma(ropeA[:, 0:T], rope_d[0, :, t0:t0 + T], writes=["ropeA"])
                P.dma(ropeA[:, T:2 * T], rope_d[1, :, t0:t0 + T], writes=["ropeA"])
                P.dma(ropeB[:, 0:T], rope_d[2, :, t0:t0 + T], writes=["ropeB"])
                P.dma(ropeB[:, T:2 * T], rope_d[3, :, t0:t0 + T], writes=["ropeB"])
                rmsnorm_T(0)
                if l == 0 and ti == 0:
                    tap("hT", hT, "hT")

                def proj_fm(ch):
                    ps, pk = psum()
                    for c in range(8):
                        mm(ps[:, :], w_in[:, c * NCOL + ch * 128: c * NCOL + (ch + 1) * 128], hT[:, c * T:(c + 1) * T],
                           c == 0, c == 7, ["w_in", "hT"], [pk])
                    return ps, pk

                for ch in range(8):
                    ps, pk = proj_fm(ch)
                    cp("pool", stage[:, 0:3], xcar[:, ch * 3:ch * 3 + 3], ["xcar"], ["stage"])
                    act(stage[:, 3:3 + T], ps[:, :], AF.Identity, [pk, "stage"], ["stage"])
                    cw = 16 + ch * 4
                    act(acc, ps[:, :], AF.Identity, [pk, "pv"], ["acc"], bias=pv[:, 48 + ch:49 + ch], scale=pv[:, cw + 3:cw + 4])
                    for j in range(3):
                        stt("dve", acc, stage[:, j:j + T], pv[:, cw + j:cw + j + 1], acc, ALU.mult, ALU.add,
                            ["stage", "acc", "pv"], ["acc"])
                    cp("pool", xcar[:, ch * 3:ch * 3 + 3], stage[:, T:T + 3], ["stage"], ["xcar"])
                    act(xbc[:, ch * T:(ch + 1) * T], acc, AF.Silu, ["acc"], ["xbc"])
                if l == 0 and ti == 0:
                    tap("xbc", xbc, "xbc")
                for c2 in range(2):
                    ps, pk = proj_fm(8 + c2)
                    act(scb[:, c2 * T:(c2 + 1) * T], ps[:, :], AF.Identity, [pk], ["scb"])
                for c2 in range(2):
                    psc, pkc = proj_fm(10 + c2)
                    psh, pkh = proj_fm(12 + c2)
                    o = c2 * (2 + T)
                    act(acc2, psc[:, :], AF.Identity, [pkc], ["acc2"])
                    cp("pool", ccs[:, o:o + 2], cccar[:, c2 * 2:c2 * 2 + 2], ["cccar"], ["ccs"])
                    tt("dve", ccs[:, o + 2:o + 2 + T], psh[:, :], acc2, ALU.mult, [pkh, "acc2", "ccs"], ["ccs"])
                    cw = 56 + c2 * 3
                    ts("dve", acc, ccs[:, o + 2:o + 2 + T], pv[:, cw + 2:cw + 3], None, ALU.mult, None, ["ccs", "pv"], ["acc"])
                    for j in range(2):
                        stt("dve", acc, ccs[:, o + j:o + j + T], pv[:, cw + j:cw + j + 1], acc, ALU.mult, ALU.add,
                            ["ccs", "acc", "pv"], ["acc"])
                    cp("pool", cccar[:, c2 * 2:c2 * 2 + 2], ccs[:, o + T:o + T + 2], ["ccs"], ["cccar"])
                    tt("dve", ymT[:, (4 + c2) * T:(5 + c2) * T], acc, scb[:, c2 * T:(c2 + 1) * T], ALU.mult,
                       ["acc", "scb"], ["ymT"])

                def roped(ch, chp, tab, tkey, out, okey):
                    ps, pk = proj_fm(ch)
                    ps2, pk2 = proj_fm(chp)
                    tt("dve", acc, ps[:, :], tab[:, 0:T], ALU.mult, [pk, tkey], ["acc"])
                    tt("dve", acc2, ps2[:, :], tab[:, T:2 * T], ALU.mult, [pk2, tkey], ["acc2"])
                    tt("pool", out, acc, acc2, ALU.add, ["acc", "acc2"], [okey])
                roped(14, 16, ropeA, "ropeA", QT[:, 0:T], "QT")
                roped(15, 17, ropeA, "ropeA", QT[:, T:2 * T], "QT")
                roped(18, 19, ropeA, "ropeA", K1r[:, t0:t0 + T], "K1r")
                ws = (ti % 2) * T
                roped(20, 21, ropeA, "ropeA", Kw[:, ws:ws + T], "Kw")
                cp("pool", K3[:, 0:16], K3[:, T:T + 16], ["K3"], ["K3"])
                roped(22, 23, ropeB, "ropeB", K3[:, 16:16 + T], "K3")
                for q4 in range(4):
                    ps, pk = psum()
                    for c in range(8):
                        mm(ps[:, :], hT[:, c * T + q4 * 128: c * T + (q4 + 1) * 128], w_in[:, c * NCOL + 3072: c * NCOL + 3584],
                           c == 0, c == 7, ["hT", "w_in"], [pk])
                    act(zs[:, q4 * 512:(q4 + 1) * 512], ps[:, :], AF.Silu, [pk], ["zs"])
                    ps, pk = psum()
                    for c in range(8):
                        mm(ps[:, 0:148], hT[:, c * T + q4 * 128: c * T + (q4 + 1) * 128], w_in[:, c * NCOL + 3584: c * NCOL + 3732],
                           c == 0, c == 7, ["hT", "w_in"], [pk])
                    tt("dve", dtt[:, q4 * 8:(q4 + 1) * 8], ps[:, 0:8], dtb, ALU.add, [pk, "rowb"], ["dtt"])
                    gt = ti * 4 + q4
                    cp("dve", Vs[:, gt * 65:gt * 65 + 64], ps[:, 8:72], [pk], ["Vs"])
                    sl = (gt % 8) * 65
                    cp("dve", Vw[:, sl:sl + 64], ps[:, 72:136], [pk], ["Vw"])
                    act(gat[:, q4 * 12:(q4 + 1) * 12], ps[:, 136:148], AF.Sigmoid, [pk], ["gat"])
                act(dtt, dtt, AF.Exp, ["dtt"], ["dtt"])
                act(dtt, dtt, AF.Ln, ["dtt"], ["dtt"], bias=1.0, scale=1.0)
                for q4 in range(4):
                    tt("dve", att[:, q4 * 8:(q4 + 1) * 8], dtt[:, q4 * 8:(q4 + 1) * 8], aneg, ALU.mult, ["dtt", "aneg"], ["att"])
                if l == 0 and ti == 0:
                    tap("dtt", dtt, "dtt")
                    tap("QT", QT, "QT")

                psbf = [p[:, :].bitcast(BF16) for p in psb]
                for q4 in range(4):
                    i = psrot[0] % 6
                    psrot[0] += 1
                    pk = "ps%d" % i
                    for c in range(6):
                        tr(psbf[i][:, c * 128:(c + 1) * 128], xbc[:, c * T + q4 * 128: c * T + (q4 + 1) * 128], ident,
                           ["xbc", "cb"], [pk])
                    cp("dve", xst[:, q4 * 512:(q4 + 1) * 512], psbf[i][:, 0:512], [pk], ["xst"])
                    act(Btk[:, q4 * 256:(q4 + 1) * 256], psbf[i][:, 512:768], AF.Identity, [pk], ["Btk"])

                P.barrier()
                for c2 in range(2):
                    tt0, tt1 = 2 * c2, 2 * c2 + 1
                    ck = c2 * 256
                    ps, pk = psum()
                    mm(ps[:, 0:8], UT, att[:, tt0 * 8:tt0 * 8 + 8], True, True, ["cf", "att"], [pk])
                    mm(ps[:, 8:16], ONES, att[:, tt0 * 8:tt0 * 8 + 8], True, False, ["cf", "att"], [pk])
                    mm(ps[:, 8:16], UT, att[:, tt1 * 8:tt1 * 8 + 8], False, True, ["cf", "att"], [pk])
                    mm(ps[:, 16:24], ONES, att[:, tt0 * 8:tt0 * 8 + 8], True, False, ["cf", "att"], [pk])
                    mm(ps[:, 16:24], ONES, att[:, tt1 * 8:tt1 * 8 + 8], False, True, ["cf", "att"], [pk])
                    cp("dve", acs, ps[:, 0:16], [pk], ["acs"])
                    cp("dve", tot, ps[:, 16:24], [pk], ["tot"])
                    if l == 0 and ti == 0 and c2 == 0:
                        tap("acs", acs, "acs")
                    act(eacs, acs, AF.Exp, ["acs"], ["eacs"])
                    for lt in range(2):
                        tt("dve", dte[:, lt * 8:lt * 8 + 8], tot, acs[:, lt * 8:lt * 8 + 8], ALU.subtract, ["tot", "acs"], ["dte"])
                    act(dte, dte, AF.Exp, ["dte"], ["dte"])
                    act(cdec, tot, AF.Exp, ["tot"], ["cdec"])
                    for lt in range(2):
                        tk = 2 * c2 + lt
                        x3 = xst[:, tk * 512:(tk + 1) * 512].rearrange("p (h d) -> p h d", d=64)
                        d3 = dtt[:, tk * 8:tk * 8 + 8].unsqueeze(2).to_broadcast([128, 8, 64])
                        tt("dve", xdtf[:, :].rearrange("p (h d) -> p h d", d=64), x3, d3, ALU.mult, ["xst", "dtt"], ["xdtf"])
                        cp("pool", xdt[:, lt * 512:(lt + 1) * 512], xdtf, ["xdtf"], ["xdt"])
                        e3 = dte[:, lt * 8:lt * 8 + 8].unsqueeze(2).to_broadcast([128, 8, 64])
                        tt("dve", xdte[:, lt * 512:(lt + 1) * 512].rearrange("p (h d) -> p h d", d=64),
                           xdtf[:, :].rearrange("p (h d) -> p h d", d=64), e3, ALU.mult, ["xdtf", "dte"], ["xdte"])
                    for g in range(2):
                        for s_t in range(2):
                            ps, pk = psum()
                            mm(ps[:, 0:256], xbc[:, (4 + g) * T + ck + s_t * 128:(4 + g) * T + ck + (s_t + 1) * 128],
                               xbc[:, (6 + g) * T + ck:(6 + g) * T + ck + 256], True, True, ["xbc"], [pk])
                            o = (g * 2 + s_t) * 256
                            act(CBT[:, o:o + 256], ps[:, 0:256], AF.Identity, [pk], ["CBT"])
                    def ssd_front(h):
                        g = h // 4
                        pb = h % 2
                        aU0, aU1, E0, E1, MT0, MT1 = aU0_[pb], aU1_[pb], E0_[pb], E1_[pb], MT0_[pb], MT1_[pb]
                        k0, k1, ke0, ke1, km0, km1 = ("aU0%d" % pb, "aU1%d" % pb, "E0%d" % pb, "E1%d" % pb, "MT0%d" % pb, "MT1%d" % pb)
                        ts("dve", aU0, U1, att[:, tt0 * 8 + h:tt0 * 8 + h + 1], None, ALU.mult, None, ["cf", "att"], [k0])
                        ts("pool", aU1, UT, att[:, tt1 * 8 + h:tt1 * 8 + h + 1], None, ALU.mult, None, ["cf", "att"], [k1])
                        ps0, pk0 = psum()
                        mm(ps0[:, 0:256], WS, aU0, True, False, ["cf", k0], [pk0])
                        mm(ps0[:, 128:256], ONES, aU1, False, False, ["cf", k1], [pk0])
                        mm(ps0[:, 0:256], ident, MC[:, 384:640], False, True, ["cb"], [pk0])
                        mm(ps0[:, 256:384], WS, aU1, True, False, ["cf", k1], [pk0])
                        mm(ps0[:, 256:384], ident, MC[:, 384:512], False, True, ["cb"], [pk0])
                        act(E0, ps0[:, 0:256], AF.Exp, [pk0], [ke0])
                        act(E1, ps0[:, 256:384], AF.Exp, [pk0], [ke1])
                        o = g * 512
                        tt("dve", MT0, E0, CBT[:, o:o + 256], ALU.mult, [ke0, "CBT"], [km0])
                        tt("pool", MT1, E1, CBT[:, o + 256 + 128:o + 512], ALU.mult, [ke1, "CBT"], [km1])

                    def ssd_back(h):
                        pb = h % 2
                        MT0, MT1 = MT0_[pb], MT1_[pb]
                        km0, km1 = "MT0%d" % pb, "MT1%d" % pb
                        hs = slice(h * 64, (h + 1) * 64)
                        mm(psb[6][:, hs], MT0[:, 0:128], xdt[:, h * 64:(h + 1) * 64], True, True, [km0, "xdt"], ["ps6"])
                        mm(psb[7][:, hs], MT0[:, 128:256], xdt[:, h * 64:(h + 1) * 64], True, False, [km0, "xdt"], ["ps7"])
                        mm(psb[7][:, hs], MT1, xdt[:, 512 + h * 64:512 + (h + 1) * 64], False, True, [km1, "xdt"], ["ps7"])
                    ssd_front(0)
                    for h in range(8):
                        if h + 1 < 8:
                            ssd_front(h + 1)
                        ssd_back(h)
                    for lt in range(2):
                        tk = 2 * c2 + lt
                        ps, pk = psum()
                        for g in range(2):
                            mm(ps[:, g * 256:(g + 1) * 256], xbc[:, (6 + g) * T + ck + lt * 128:(6 + g) * T + ck + (lt + 1) * 128],
                               HTb[:, g * 256:(g + 1) * 256], True, True, ["xbc", "HTb"], [pk])
                        e3 = eacs[:, lt * 8:lt * 8 + 8].unsqueeze(2).to_broadcast([128, 8, 64])
                        tt("dve", ysb[:, :].rearrange("p (h d) -> p h d", d=64), ps[:, :].rearrange("p (h d) -> p h d", d=64), e3,
                           ALU.mult, [pk, "eacs"], ["ysb"])
                        tt("dve", ysb, ysb, psb[6 + lt][:, :], ALU.add, ["ysb", "ps%d" % (6 + lt)], ["ysb"])
                        if l == 0 and ti == 0 and c2 == 0 and lt == 0:
                            tap("ydiag", ysb, "ysb")
                        D3 = Dsk.unsqueeze(2).to_broadcast([128, 8, 64])
                        tt("pool", ysb2[:, :].rearrange("p (h d) -> p h d", d=64),
                           xst[:, tk * 512:(tk + 1) * 512].rearrange("p (h d) -> p h d", d=64), D3, ALU.mult, ["xst", "rowb"], ["ysb2"])
                        tt("dve", ysb, ysb, ysb2, ALU.add, ["ysb", "ysb2"], ["ysb"])
                        tt("dve", ysb, ysb, zs[:, tk * 512:(tk + 1) * 512], ALU.mult, ["ysb", "zs"], ["ysb"])
                        if l == 0 and ti == 0 and c2 == 0 and lt == 0:
                            tap("yssd", ysb, "ysb")
                        mset("dve", sm1[:, 0:1], 0.0, ["sm1"])
                        act(ysb2, ysb, AF.Square, ["ysb", "sm1"], ["ysb2", "sm1"], accum=sm1[:, 0:1])
                        act(sm1[:, 1:2], sm1[:, 0:1], AF.Sqrt, ["sm1"], ["sm1"], bias=EPS_AP, scale=1.0 / 512.0)
                        P.op("dve", lambda e: e.reciprocal(out=sm1[:, 2:3], in_=sm1[:, 1:2]), ["sm1"], ["sm1"])
                        stt("dve", ynb, ysb, sm1[:, 2:3], nrmw, ALU.mult, ALU.mult, ["ysb", "sm1", "rowb"], ["ynb"])
                        i = psrot[0] % 6
                        psrot[0] += 1
                        pk2 = "ps%d" % i
                        for c in range(4):
                            tr(psbf[i][:, c * 128:(c + 1) * 128], ynb[:, c * 128:(c + 1) * 128], ident, ["ynb", "cb"], [pk2])
                        for c in range(4):
                            o = c * T + ck + lt * 128
                            if c % 2:
                                cp("dve", ymT[:, o:o + 128], psbf[i][:, c * 128:(c + 1) * 128], [pk2], ["ymT"])
                            else:
                                act(ymT[:, o:o + 128], psbf[i][:, c * 128:(c + 1) * 128], AF.Identity, [pk2], ["ymT"])
                    for g in range(2):
                        ps, pk = psum()
                        for lt in range(2):
                            tk = 2 * c2 + lt
                            mm(ps[:, 0:256], Btk[:, tk * 256 + g * 128: tk * 256 + (g + 1) * 128],
                               xdte[:, lt * 512 + g * 256: lt * 512 + (g + 1) * 256], lt == 0, lt == 1, ["Btk", "xdte"], [pk])
                        for k in range(4):
                            h = g * 4 + k
                            stt("dve", HTs[:, h * 64:(h + 1) * 64], HTs[:, h * 64:(h + 1) * 64], cdec[:, h:h + 1],
                                ps[:, k * 64:(k + 1) * 64], ALU.mult, ALU.add, ["HTs", "cdec", pk], ["HTs"])
                    cp("pool", HTb, HTs, ["HTs"], ["HTb"])

                P.barrier()
                for half in range(2):
                    r0 = half * 64
                    ps, pk = psum()
                    for j in range(32):
                        mm(ps[:, 0:32], w1r[r0:r0 + 64, j * 128:(j + 1) * 128], K3[r0:r0 + 64, j:j + 16 * 31 + 1:16],
                           j == 0, j == 31, ["w1r", "K3"], [pk])
                    if half == 0:
                        act(hidb[:, 0:32], ps[:, 0:32], AF.Gelu_apprx_tanh, [pk, "cbias"], ["hidb"],
                            bias=cbias[:, 0:1], scale=1.0)
                    else:
                        mset("pool", hidv, 0.0, ["hidv"])
                        pa_ = 32 * (ti % 4)
                        act(hidv[:, pa_:pa_ + 32], ps[:, 0:32], AF.Gelu_apprx_tanh, [pk, "cbias", "hidv"], ["hidv"],
                            bias=cbias[:, 1:2], scale=1.0)
                ps, pk = psum()
                mm(ps[:, 0:32], w2k, hidb[:, 0:32], True, True, ["w2k", "hidb"], [pk])
                cp("dve", kcT[:, 32 * ti:32 * ti + 32], ps[:, 0:32], [pk], ["kcT"])
                pa = 32 * (ti % 4)
                ps, pk = psum()
                mm(ps[:, 0:64], hidv[:, :], w2v, True, True, ["w2v", "hidv"], [pk])
                o = (ti // 4) * 64
                tt("dve", vcc[:, o:o + 64], vcc[:, o:o + 64], ps[:, 0:64], ALU.add, [pk, "vcc"], ["vcc"])
                if ti == 0:
                    mset("dve", kcT[:, 0:1], 0.0, ["kcT"])
                if 4 * ti + 3 >= 48:
                    mset("pool", HW96[0:96, :], 0.0, ["HW96"])
                    cp("pool", HW96[96:128, :], HW32[96:128, :], ["cb", "HW96"], ["HW96"])

                def cmp_front(qb, h):
                    qbg = ti * 4 + qb
                    Wc = 8 * (qbg + 1)
                    hp = (h % 2) * 64
                    qch = h // 2
                    ps, pk = psum()
                    mm(ps[:, 0:Wc], QT[hp:hp + 64, qch * T + qb * 128: qch * T + (qb + 1) * 128], kcT[hp:hp + 64, 0:Wc],
                       True, True, ["QT", "kcT"], [pk])
                    pc = Pc[h % 2]
                    pck = "Pc%d" % (h % 2)
                    act(pc[:, 0:Wc], ps[:, 0:Wc], AF.Exp, [pk], [pck], scale=0.125)
                    tt("dve", pc[:, Wc - 8:Wc], pc[:, Wc - 8:Wc], PB, ALU.mult, [pck, "cf"], [pck])
                    mset("dve", pc[:, 0:1], 0.0, [pck])
                    P.op("dve", lambda e: e.reduce_sum(out=rs[:, h:h + 1], in_=pc[:, 0:Wc], axis=AX.X), [pck], ["rs"])
                    ts("dve", rs[:, 4 + h:5 + h], rs[:, h:h + 1], 1e-20, None, ALU.max, None, ["rs"], ["rs"])
                    P.op("dve", lambda e: e.reciprocal(out=rs[:, 4 + h:5 + h], in_=rs[:, 4 + h:5 + h]), ["rs"], ["rs"])
                    ts("dve", pc[:, 0:Wc], pc[:, 0:Wc], rs[:, 4 + h:5 + h], None, ALU.mult, None, [pck, "rs"], [pck])
                    if h == 0:
                        mset("pool", Psum_, 0.0, ["Psum"])
                        cp("pool", Psum_[:, 0:Wc], pc[:, 0:Wc], [pck, "Psum"], ["Psum"])
                    else:
                        tt("pool", Psum_[:, 0:Wc], Psum_[:, 0:Wc], pc[:, 0:Wc], ALU.add, [pck, "Psum"], ["Psum"])
                    ib = (qb * 4 + h) % 2
                    cp("pool", Pcb_[ib][:, 0:Wc], pc[:, 0:Wc], [pck], ["Pcb%d" % ib])

                def cmp_back(qb, h):
                    qbg = ti * 4 + qb
                    Wc = 8 * (qbg + 1)
                    nch = (Wc + 127) // 128
                    ib = (qb * 4 + h) % 2
                    Pcb = Pcb_[ib]
                    i = psrot[0] % 6
                    psrot[0] += 1
                    pk2 = "ps%d" % i
                    for c in range(nch):
                        w = min(128, Wc - c * 128)
                        tr(psbf[i][0:w, c * 128:(c + 1) * 128], Pcb[:, c * 128:c * 128 + w], ident, ["Pcb%d" % ib, "cb"], [pk2])
                    for c in range(nch):
                        w = min(128, Wc - c * 128)
                        cp("dve", PcT[0:w, c * 128:(c + 1) * 128], psbf[i][0:w, c * 128:(c + 1) * 128], [pk2], ["PcT"])
                    ps, pk = psum()
                    for c in range(nch):
                        w = min(128, Wc - c * 128)
                        mm(ps[:, 0:64], PcT[0:w, c * 128:(c + 1) * 128], vcc[0:w, c * 64:(c + 1) * 64], c == 0, c == nch - 1,
                           ["PcT", "vcc"], [pk])
                    ts("dve", yq[:, qb * 256 + h * 64:qb * 256 + (h + 1) * 64], ps[:, 0:64],
                       gat[:, qb * 12 + h * 3:qb * 12 + h * 3 + 1], None, ALU.mult, None, [pk, "gat"], ["yq"])

                def cmp_topk(qb):
                    qbg = ti * 4 + qb
                    qs = slice(qb * 128, (qb + 1) * 128)
                    P.op("dve", lambda e: e.tensor_reduce(out=impb, in_=Psum_[:, 0:512].rearrange("p (j k) -> p j k", k=4),
                                                          axis=AX.X, op=ALU.add), ["Psum"], ["impb"])
                    tt("dve", impb, impb, Psum_[:, 4:516:4], ALU.add, ["impb", "Psum"], ["impb"])
                    Wb = 2 * qbg + 2
                    mset("pool", Vb, -1e30, ["Vb"])
                    cp("dve", Vb[:, 0:Wb], impb[:, 0:Wb], ["impb", "Vb"], ["Vb"])
                    j0 = 2 * qbg
                    if j0 - 1 >= 0:
                        ts("dve", Vb[:, j0 - 1:j0], Vb[:, j0 - 1:j0], FPt[:, 0:1], FPt[:, 1:2], ALU.mult, ALU.add, ["Vb", "cf"], ["Vb"])
                    mset("dve", Vb[:, j0:j0 + 1], 1e9, ["Vb"])
                    cp("dve", Vb[:, j0 + 1:j0 + 2], FPt[:, 2:3], ["cf", "Vb"], ["Vb"])
                    mset("dve", Vb[:, 0:1], 1e9, ["Vb"])
                    if Wb >= 18:
                        P.op("dve", lambda e: e.max(out=m8[:, 0:8], in_=Vb[:, :]), ["Vb"], ["m8"])
                        P.op("dve", lambda e: e.match_replace(out=Vb2[:, :], in_to_replace=m8[:, 0:8], in_values=Vb[:, :], imm_value=-3e38),
                             ["Vb", "m8"], ["Vb2"])
                        P.op("dve", lambda e: e.max(out=m8[:, 8:16], in_=Vb2[:, :]), ["Vb2"], ["m8"])
                        ts("dve", Vb2, Vb, m8[:, 15:16], None, ALU.is_ge, None, ["Vb", "m8"], ["Vb2"])
                    else:
                        ts("dve", Vb2, Vb, -1e29, None, ALU.is_ge, None, ["Vb"], ["Vb2"])
                    ts("dve", selb, Vb2, -1.0, BIG, ALU.add, ALU.mult, ["Vb2"], ["selb"])
                    i = psrot[0] % 6
                    psrot[0] += 1
                    pk2 = "ps%d" % i
                    tr(psbf[i][:, 0:128], selb, ident, ["selb", "cb"], [pk2])
                    cp("dve", selbT[:, qs], psbf[i][:, 0:128], [pk2], ["selbT"])

                its = [(qb, h) for qb in range(4) for h in range(4)]
                cmp_front(*its[0])
                for n_, (qb, h) in enumerate(its):
                    if n_ + 1 < len(its):
                        if its[n_ + 1][1] == 0:
                            cmp_topk(qb)
                        cmp_front(*its[n_ + 1])
                    cmp_back(qb, h)
                cmp_topk(3)
                if l == 0 and ti == 0:
                    tap("yq_c", yq, "yq")
                    tap("kcT", kcT, "kcT")
                    tap("selbT", selbT, "selbT")
                for br in range(2):
                    for h in range(4):
                        hp = (h % 2) * 64
                        qch = h // 2
                        qT_h = QT[hp:hp + 64, qch * T:(qch + 1) * T]
                        if br == 0:
                            kts = list(range(0, 4 * ti + 4))
                        else:
                            kts = [k for k in range(4 * ti - 4, 4 * ti + 4) if k >= 0]
                        oacc = psb[6 + br]
                        oak = "ps%d" % (6 + br)
                        for n, kt in enumerate(kts):
                            ps, pk = psum()
                            if br == 0:
                                mm(ps[:, :], K1r[hp:hp + 64, kt * 128:(kt + 1) * 128], qT_h, True, False, ["K1r", "QT"], [pk])
                                a32 = (2 * kt) // 32
                                kk = kt % 16
                                d = kt - 4 * ti
                                if a32 < 3:
                                    mm(ps[:, :], HW32[32 * a32:32 * a32 + 32, kk * 128:(kk + 1) * 128], selbT[32 * a32:32 * a32 + 32, :],
                                       False, d < 0, ["cb", "selbT"], [pk])
                                else:
                                    mm(ps[:, :], HW96[:, kk * 128:(kk + 1) * 128], selbT[:, :],
                                       False, d < 0, ["HW96", "selbT"], [pk])
                                if d >= 0:
                                    mm(ps[:, :], ident, MC[:, 384 - 128 * d:384 - 128 * d + 512], False, True, ["cb"], [pk])
                            else:
                                slot = ((kt // 4) % 2) * T + (kt % 4) * 128
                                mm(ps[:, :], Kw[hp:hp + 64, slot:slot + 128], qT_h, True, False, ["Kw", "QT"], [pk])
                                j = kt - (4 * ti - 4)
                                mm(ps[:, :], ident, MW[:, 896 - 128 * j:896 - 128 * j + 512], False, True, ["cb"], [pk])
                            pt = PT[n % 3]
                            ptk = "PT%d" % (n % 3)
                            act(pt, ps[:, :], AF.Exp, [pk], [ptk], scale=0.125)
                            if l == 0 and ti == 0 and br == 0 and h == 0 and n < 2:
                                tap("PT%d" % n, pt, ptk)
                            for qb in range(4):
                                if br == 0:
                                    vv = Vs[:, kt * 65:(kt + 1) * 65]
                                    vk = "Vs"
                                else:
                                    vv = Vw[:, (kt % 8) * 65:(kt % 8 + 1) * 65]
                                    vk = "Vw"
                                mmg(oacc[:, qb * 128:qb * 128 + 65], pt[:, qb * 128:(qb + 1) * 128], vv, n == 0 and qb == 0,
                                    n == len(kts) - 1, [ptk, vk], [oak])
                        for qb in range(4):
                            ts("dve", rs[:, 0:1], oacc[:, qb * 128 + 64:qb * 128 + 65], 1e-30, None, ALU.max, None, [oak], ["rs"])
                            P.op("dve", lambda e: e.reciprocal(out=rs[:, 1:2], in_=rs[:, 0:1]), ["rs"], ["rs"])
                            gi = qb * 12 + h * 3 + 1 + br
                            tt("dve", rs[:, 1:2], rs[:, 1:2], gat[:, gi:gi + 1], ALU.mult, ["rs", "gat"], ["rs"])
                            yv = yq[:, qb * 256 + h * 64:qb * 256 + (h + 1) * 64]
                            stt("dve", yv, oacc[:, qb * 128:qb * 128 + 64], rs[:, 1:2], yv, ALU.mult, ALU.add, [oak, "rs", "yq"], ["yq"])
                if l == 0 and ti == 0:
                    tap("yq", yq, "yq")
                for qb in range(4):
                    cp("pool", ynsa, yq[:, qb * 256:(qb + 1) * 256], ["yq"], ["ynsa"])
                    i = psrot[0] % 6
                    psrot[0] += 1
                    pk2 = "ps%d" % i
                    for c in range(2):
                        tr(psbf[i][:, c * 128:(c + 1) * 128], ynsa[:, c * 128:(c + 1) * 128], ident, ["ynsa", "cb"], [pk2])
                    for c in range(2):
                        o = (6 + c) * T + qb * 128
                        cp("dve", ymT[:, o:o + 128], psbf[i][:, c * 128:(c + 1) * 128], [pk2], ["ymT"])
                if l == 0 and ti == 0:
                    tap("ymT", ymT, "ymT")

                P.barrier()
                for co in range(8):
                    ps, pk = psum()
                    for c in range(8):
                        mm(ps[:, :], w_out[:, c * 1024 + co * 128: c * 1024 + (co + 1) * 128], ymT[:, c * T:(c + 1) * T],
                           c == 0, c == 7, ["w_out", "ymT"], [pk])
                    tt("dve", xT[:, co * T:(co + 1) * T], xT[:, co * T:(co + 1) * T], ps[:, :], ALU.add, ["xT", pk], ["xT"])
                for c in range(8):
                    P.dma(X2[:, c, t0:t0 + T], xT[:, c * T:(c + 1) * T], reads=["xT"], writes=["X2"])

            P.barrier()
            A.off = base_off
            hT = A.bf(8 * T)
            w_up = A.bf(8 * 5504)
            w_dn = A.bf(22 * 1024)
            ucar = A.f32(44 * 2)
            ust = [[A.f32(2 + T) for _ in range(2)] for _ in range(2)]
            fa = [[A.f32(T) for _ in range(2)] for _ in range(2)]
            NRM.update(sq=[(fa[0][0], "fa00"), (fa[1][0], "fa10")], rstd=(ust[0][0][:, 0:T], "ust00"))
            aT = A.bf(22 * T)
            print('phaseB arena', A.off)
            for c in range(8):
                P.dma(w_up[:, c * 5504:(c + 1) * 5504], wup_d[l, :, c, :], writes=["w_up"], eng="pool")
            for c in range(3):
                P.dma(w_dn[:, c * 7168:(c + 1) * 7168].rearrange("p (k n) -> p k n", n=1024),
                      wdn_d[l, c * 896:(c + 1) * 896, :].rearrange("(k p) n -> p k n", p=128), writes=["w_dn"], eng="pool")
            P.dma(w_dn[0:64, 21 * 1024:22 * 1024], wdn_d[l, 2688:2752, :], writes=["w_dn"], eng="pool")
            mset("dve", ucar, 0.0, ["ucar"])
            last = (l == NL - 1)
            for ti in range(NT):
                t0 = ti * T
                for c in range(8):
                    P.dma(xT[:, c * T:(c + 1) * T], X2[:, c, t0:t0 + T], reads=["X2"], writes=["xT"])
                rmsnorm_T(8)
                for j in range(22):
                    np_ = 128 if j < 21 else 64
                    res = []
                    for half in range(2):
                        col = half * D_FF + j * 128
                        ps, pk = psum()
                        for c in range(8):
                            mm(ps[0:np_, :], w_up[:, c * 5504 + col: c * 5504 + col + np_], hT[:, c * T:(c + 1) * T],
                               c == 0, c == 7, ["w_up", "hT"], [pk])
                        par = j % 2
                        u = ust[half][par]
                        uk = "ust%d%d" % (half, par)
                        ci = half * 22 + j
                        cp("pool", u[0:np_, 0:2], ucar[0:np_, ci * 2:ci * 2 + 2], ["ucar"], [uk])
                        act(u[0:np_, 2:2 + T], ps[0:np_, :], AF.Identity, [pk, uk], [uk])
                        cw = 62 + half * 66 + j * 3
                        bo = 194 + half * 22 + j
                        f = fa[half][par]
                        fk = "fa%d%d" % (half, par)
                        act(f[0:np_, :], ps[0:np_, :], AF.Identity, [pk, "pv"], [fk], bias=pv[0:np_, bo:bo + 1], scale=pv[0:np_, cw + 2:cw + 3])
                        for k in range(2):
                            stt("dve" if half == 0 else "pool", f[0:np_, :], u[0:np_, k:k + T], pv[0:np_, cw + k:cw + k + 1], f[0:np_, :],
                                ALU.mult, ALU.add, [uk, fk, "pv"], [fk])
                        cp("pool", ucar[0:np_, ci * 2:ci * 2 + 2], u[0:np_, T:T + 2], [uk], ["ucar"])
                    par = j % 2
                    act(fa[0][par][0:np_, :], fa[0][par][0:np_, :], AF.Silu, ["fa0%d" % par], ["fa0%d" % par])
                    tt("pool", aT[0:np_, j * T:(j + 1) * T], fa[0][par][0:np_, :], fa[1][par][0:np_, :], ALU.mult,
                       ["fa0%d" % par, "fa1%d" % par], ["aT"])
                for co in range(8):
                    ps, pk = psum()
                    for j in range(22):
                        np_ = 128 if j < 21 else 64
                        mm(ps[:, :], w_dn[0:np_, j * 1024 + co * 128: j * 1024 + (co + 1) * 128], aT[0:np_, j * T:(j + 1) * T],
                           j == 0, j == 21, ["w_dn", "aT"], [pk])
                    tt("dve", xT[:, co * T:(co + 1) * T], xT[:, co * T:(co + 1) * T], ps[:, :], ALU.add, ["xT", pk], ["xT"])
                if not last:
                    for c in range(8):
                        P.dma(X1[:, c, t0:t0 + T], xT[:, c * T:(c + 1) * T], reads=["xT"], writes=["X1"])
                else:
                    rb, rk = rms_stats()
                    for c in range(8):
                        o_ = fa[c % 2][1]
                        ok_ = "fa%d1" % (c % 2)
                        stt("dve", o_, xT[:, c * T:(c + 1) * T], fnw[:, c:c + 1], rb, ALU.mult, ALU.mult, ["xT", rk, "fnw"], [ok_])
                        P.dma(outT_d[:, c, t0:t0 + T], o_, reads=[ok_], writes=["outT"])
        P.emit(final_keys=["outT"] + ["tap_" + n for n in tap_d])
    return nc


IN_OFF = dict(z=0, xbc=512, dt=1536, sc_b=1544, sc_c=1800, sc_h=2056, q=2312, k_c=2568, v_c=2632,
              k_s=2696, v_s=2760, k_w=2824, v_w=2888, g=2952)


def _cols():
    r = np.arange
    perm = (r(64) + 32) % 64
    o = IN_OFF
    cols = [r(o["xbc"], o["xbc"] + 1024), r(o["sc_b"], o["sc_b"] + 768), r(o["q"], o["q"] + 256)]
    cols.append(np.concatenate([o["q"] + h * 64 + perm for h in range(4)]))
    for k in ("k_s", "k_w"):
        cols += [r(o[k], o[k] + 64), r(o[k], o[k] + 64), o[k] + perm, o[k] + perm]
    cols += [r(o["k_c"], o["k_c"] + 64), r(o["v_c"], o["v_c"] + 64), o["k_c"] + perm, r(o["v_c"], o["v_c"] + 64)]
    cols += [r(o["z"], o["z"] + 512), r(o["dt"], o["dt"] + 8), r(o["v_s"], o["v_s"] + 64), r(o["v_w"], o["v_w"] + 64),
             r(o["g"], o["g"] + 12)]
    c = np.concatenate(cols)
    assert c.shape[0] == NCOL
    return c


def _fm(w):
    return np.ascontiguousarray(w.reshape(8, 128, -1).transpose(1, 0, 2))


def _pc(v, n):
    return v.reshape(n, 128).T


def _pad(v, n):
    out = np.zeros(v.shape[:-1] + (n,), v.dtype)
    out[..., :v.shape[-1]] = v
    return out


def consts(S):
    f32 = np.float32
    half = 32
    inv = (1.0 / (10000.0 ** (np.arange(half, dtype=f32) / f32(half)))).astype(f32)
    ang = (np.arange(S, dtype=f32)[:, None] * inv[None, :]).astype(f32)
    cos = np.cos(ang).astype(f32).T
    sin = np.sin(ang).astype(f32).T
    cos64 = np.concatenate([cos, cos], 0)
    sin64 = np.concatenate([-sin, sin], 0)
    rope = np.stack([np.concatenate([cos64, cos64], 0), np.concatenate([sin64, sin64], 0),
                     np.concatenate([cos64, np.ones_like(cos64)], 0), np.concatenate([sin64, np.zeros_like(sin64)], 0)], 0)
    p = np.arange(128)[:, None]
    cstb = np.zeros((128, 4480), f32)
    cstb[:, 0:128] = np.eye(128)
    c = np.arange(896)[None, :]
    cstb[:, 128:1024] = np.where((c - 384) - p >= 0, 0.0, -BIG)
    c = np.arange(1408)[None, :]
    u = p - (c - 896)
    cstb[:, 1024:2432] = np.where((u > 0) & (u <= 512), 0.0, -BIG)
    c = np.arange(2048)[None, :]
    cstb[:, 2432:4480] = ((p % 32) == 2 * (c // 128) + (c % 128) // 64).astype(f32)
    cstf = np.zeros((128, 523), f32)
    k = np.arange(128)[:, None]
    l_ = np.arange(128)[None, :]
    cstf[:, 0:128] = (k <= l_)
    cstf[:, 128:256] = 1.0
    cstf[:, 256:384] = (k > l_)
    m = np.arange(8)[None, :]
    cstf[:, 384:392] = (16 * m + 15 <= p)
    lo = (np.arange(128) < 64)
    cstf[:, 392] = np.where(lo, 0.0, 1.0)
    cstf[:, 393] = np.where(lo, 1e9, 0.0)
    cstf[:, 394] = np.where(lo, -1e30, 1e9)
    cstf[:, 395:523] = np.eye(128)
    return rope.astype(f32), cstb, cstf


def prep_shared(inp, NL):
    f32 = np.float32
    cols = _cols()
    d = {}
    d["w_in"] = np.stack([_fm(np.asarray(inp["w_in"][l])[:, cols]) for l in range(NL)])
    d["w_out"] = np.stack([_fm(np.asarray(inp["w_out"][l])) for l in range(NL)])
    d["w_up"] = np.stack([_fm(np.asarray(inp["ffn_w_up"][l])) for l in range(NL)])
    d["w_dn"] = np.ascontiguousarray(np.asarray(inp["ffn_w_down"][:NL]))
    pv = np.zeros((NL, 128, 238), f32)
    rowv = np.zeros((NL, 1, 536), f32)
    w1r = np.zeros((NL, 128, 32, 128), f32)
    posT = np.zeros((NL, 128, 32), f32)
    w2k = np.zeros((NL, 128, 128), f32)
    w2v = np.zeros((NL, 128, 64), f32)
    for l in range(NL):
        pv[l, :, 0:8] = _pc(inp["attn_norm_w"][l], 8)
        pv[l, :, 8:16] = _pc(inp["ffn_norm_w"][l], 8)
        pv[l, :, 16:48] = inp["ssd_conv_w"][l].reshape(4, 8, 128).transpose(2, 1, 0).reshape(128, 32)
        pv[l, :, 48:56] = _pc(inp["ssd_conv_b"][l], 8)
        pv[l, :, 56:62] = inp["sc_conv_w"][l].reshape(3, 2, 128).transpose(2, 1, 0).reshape(128, 6)
        cw = inp["ffn_conv_w"][l]
        cbv = inp["ffn_conv_b"][l]
        for hf in range(2):
            w = _pad(cw[:, hf * D_FF:(hf + 1) * D_FF], 2816)
            pv[l, :, 62 + hf * 66:62 + (hf + 1) * 66] = w.reshape(3, 22, 128).transpose(2, 1, 0).reshape(128, 66)
            pv[l, :, 194 + hf * 22:194 + (hf + 1) * 22] = _pc(_pad(cbv[hf * D_FF:(hf + 1) * D_FF], 2816), 22)
        rowv[l, 0, 0:8] = inp["ssd_dt_bias"][l]
        rowv[l, 0, 8:16] = inp["ssd_a_log"][l]
        rowv[l, 0, 16:24] = inp["ssd_d"][l]
        rowv[l, 0, 24:536] = inp["ssd_norm_w"][l]
        w1r[l, 0:64] = inp["cmp_k_w1"][l].reshape(32, 64, 128).transpose(1, 0, 2)
        w1r[l, 64:128] = inp["cmp_v_w1"][l].reshape(32, 64, 128).transpose(1, 0, 2)
        posT[l, 0:64] = inp["cmp_k_pos"][l].T
        posT[l, 64:128] = inp["cmp_v_pos"][l].T
        w2k[l] = np.concatenate([inp["cmp_k_w2"][l], inp["cmp_k_w2"][l]], 1)
        w2v[l] = inp["cmp_v_w2"][l]
    d.update(pv=pv, rowv=rowv, w1r=w1r, posT=posT, w2k=w2k, w2v=w2v)
    d["fnw"] = np.ascontiguousarray(_pc(np.asarray(inp["final_norm_w"]), 8))
    return d


_NC_CACHE = {}


def run(inp, S, NL, nb, taps=None, n_cores=None):
    inp = {k: np.asarray(v, dtype=np.float32) for k, v in inp.items()}
    key = (S, NL, tuple(sorted(taps)) if taps else None)
    if key not in _NC_CACHE:
        _NC_CACHE[key] = build(S, NL, taps)
    nc = _NC_CACHE[key]
    shared = prep_shared(inp, NL)
    rope, cstb, cstf = consts(S)
    shared.update(rope=rope, cstb=cstb, cstf=cstf)
    n_cores = n_cores or 8
    in_maps = []
    for core in range(n_cores):
        b = core % nb
        x = inp["x"][b, :S]
        m = dict(shared)
        m["xT"] = np.ascontiguousarray(x.T.reshape(8, 128, S).transpose(1, 0, 2))
        in_maps.append(m)
    res = run_bass_kernel_spmd(nc, in_maps, core_ids=list(range(n_cores)))
    outs = []
    for b in range(nb):
        o = res.results[b]["outT"]
        outs.append(o.transpose(2, 1, 0).reshape(S, 1024))
    return np.stack(outs), res


def kernel(**inputs):
    out, _ = run(inputs, 8192, 4, 4)
    return out.astype(np.float32)
```

```python
import numpy as np
from contextlib import ExitStack
import concourse.bass as bass
import concourse.mybir as mybir
from concourse.bass_utils import run_bass_kernel_spmd

F32 = mybir.dt.float32
BF16 = mybir.dt.bfloat16
AF = mybir.ActivationFunctionType
ALU = mybir.AluOpType
AX = mybir.AxisListType

BIG = 30000.0
import os
SKIP96 = bool(int(os.environ.get('SKIP96', '0')))
T = 512
NCOL = 3732
D_FF = 2752
EPS = 1e-6


class Prog:
    ENG = ("pe", "dve", "act", "pool", "sp")
    NDMA = 8
    EPOCH = 16000

    def __init__(self, nc, stack):
        self.nc = nc
        self.stack = stack
        self.streams = {e: [] for e in self.ENG}
        self.count = {e: 0 for e in self.ENG}
        self.sems = {}
        self.waited = {e: {} for e in self.ENG}
        self.last_w = {}
        self.readers = {}
        self.ndma = 0
        self.dma_tok = {}
        self.ndma_q = {}

    def _deps(self, reads, writes):
        deps = {}

        def add(s, v):
            if deps.get(s, 0) < v:
                deps[s] = v
        for k in reads:
            t = self.last_w.get(k)
            if t is not None:
                add(*t)
        for k in writes:
            t = self.last_w.get(k)
            if t is not None:
                add(*t)
            for s, v in self.readers.get(k, {}).items():
                add(s, v)
        return deps

    def _commit(self, tok, reads, writes):
        s, v = tok
        for k in reads:
            r = self.readers.setdefault(k, {})
            if r.get(s, 0) < v:
                r[s] = v
        for k in writes:
            self.last_w[k] = tok
            self.readers[k] = {}

    def _waits(self, eng, deps):
        waits = []
        for s, v in deps.items():
            if eng == "pe" and s[0] == "pe":
                continue
            if self.waited[eng].get(s, 0) < v:
                self.waited[eng][s] = v
                waits.append((s, v))
        return waits

    def op(self, eng, fn, reads=(), writes=()):
        deps = self._deps(reads, writes)
        self.count[eng] += 1
        c = self.count[eng]
        sem = (eng, (c - 1) // self.EPOCH)
        val = (c - 1) % self.EPOCH + 1
        self.streams[eng].append((self._waits(eng, deps), fn, sem, val, 1))
        self._commit((sem, val), reads, writes)

    def dma(self, out, in_, reads=(), writes=(), eng="sp"):
        deps = self._deps(reads, writes)
        cnt = self.ndma_q.get(eng, 0)
        self.ndma_q[eng] = cnt + 1
        j = cnt % self.NDMA
        r = cnt // self.NDMA
        self.ndma += 1
        sem = ("dma" + eng, j)
        if r > 0:
            deps[sem] = max(deps.get(sem, 0), 16 * r)
        fn = lambda e: e.dma_start(out=out, in_=in_)
        self.streams[eng].append((self._waits(eng, deps), fn, sem, 16 * (r + 1), 16))
        self._commit((sem, 16 * (r + 1)), reads, writes)

    def barrier(self):
        toks = {}
        for e in self.ENG:
            c = self.count[e]
            if c > 0:
                toks[(e, (c - 1) // self.EPOCH)] = (c - 1) % self.EPOCH + 1
        for q, cnt in self.ndma_q.items():
            for j in range(min(cnt, self.NDMA)):
                n = (cnt - 1 - j) // self.NDMA + 1
                toks[("dma" + q, j)] = 16 * n
        for e in self.ENG:
            w = self._waits(e, dict(toks))
            if w:
                self.streams[e].append((w, None, None, 0, 0))

    def emit(self, final_keys=()):
        nc = self.nc
        fw = list(self._deps(final_keys, ()).items())
        names = set()
        for e in self.ENG:
            for (waits, fn, sem, val, inc) in self.streams[e]:
                if sem is not None:
                    names.add(sem)
                for s, v in waits:
                    names.add(s)
        for s, v in fw:
            names.add(s)
        for s in sorted(names, key=str):
            self.sems[s] = self.stack.enter_context(nc.semaphore("s_" + "_".join(str(x) for x in s)))
        engobj = {"pe": "tensor", "dve": "vector", "act": "scalar", "pool": "gpsimd", "sp": "sync"}
        with nc.Block() as block:
            for e in self.ENG:
                stream = self.streams[e]
                if not stream and e != "sp":
                    continue

                def body(eo, stream=stream, e=e):
                    for (waits, fn, sem, val, inc) in stream:
                        for s, v in waits:
                            eo.wait_ge(self.sems[s], v)
                        if fn is not None:
                            fn(eo).then_inc(self.sems[sem], inc)
                    if e == "sp":
                        for s, v in fw:
                            eo.wait_ge(self.sems[s], v)
                getattr(block, engobj[e])(body)


class Arena:
    def __init__(self, buf, n):
        self.buf = buf
        self.n = n
        self.off = 0

    def bf(self, cols):
        a = self.buf[:, self.off:self.off + cols]
        self.off += (cols + 15) // 16 * 16
        assert self.off <= self.n, ("arena overflow", self.off, self.n)
        return a

    def f32(self, cols):
        a = self.buf[:, self.off:self.off + 2 * cols].bitcast(F32)
        self.off += (2 * cols + 15) // 16 * 16
        assert self.off <= self.n, ("arena overflow", self.off, self.n)
        return a


def build(S, NL, taps=None, NAR_=106400):
    NT = S // T
    nc = bass.Bass("TRN2", target_bir_lowering=False)

    def Din(name, shape):
        return nc.dram_tensor(name, shape, F32, kind="ExternalInput").ap()
    xT_d = Din("xT", [128, 8, S])
    win_d = Din("w_in", [NL, 128, 8, NCOL])
    wout_d = Din("w_out", [NL, 128, 8, 1024])
    wup_d = Din("w_up", [NL, 128, 8, 5504])
    wdn_d = Din("w_dn", [NL, D_FF, 1024])
    pv_d = Din("pv", [NL, 128, 238])
    rowv_d = Din("rowv", [NL, 1, 536])
    w1r_d = Din("w1r", [NL, 128, 32, 128])
    posT_d = Din("posT", [NL, 128, 32])
    w2k_d = Din("w2k", [NL, 128, 128])
    w2v_d = Din("w2v", [NL, 128, 64])
    fnw_d = Din("fnw", [128, 8])
    rope_d = Din("rope", [4, 128, S])
    cstb_d = Din("cstb", [128, 4480])
    cstf_d = Din("cstf", [128, 523])
    outT_d = nc.dram_tensor("outT", [128, 8, S], F32, kind="ExternalOutput").ap()
    X1 = nc.dram_tensor("X1", [128, 8, S], F32).ap()
    X2 = nc.dram_tensor("X2", [128, 8, S], F32).ap()
    tap_d = {}
    if taps:
        for name, shape in taps.items():
            tap_d[name] = nc.dram_tensor("tap_" + name, shape, F32, kind="ExternalOutput").ap()

    with ExitStack() as st:
        P = Prog(nc, st)
        NAR = NAR_
        arena_t = st.enter_context(nc.sbuf_tensor("arena", [128, NAR], BF16))
        A = Arena(arena_t, NAR)
        psb = [st.enter_context(nc.psum_tensor("ps%d" % i, [128, 512], F32)) for i in range(8)]
        psrot = [0]

        def psum():
            i = psrot[0] % 6
            psrot[0] += 1
            return psb[i], "ps%d" % i

        def mm(out, lhsT, rhs, start, stop, reads, writes):
            P.op("pe", lambda e: e.matmul(out, lhsT=lhsT, rhs=rhs, start=start, stop=stop), reads, writes)

        def mmg(out, lhsT, rhs, start, stop, reads, writes):
            P.op("pe", lambda e: e.matmul(out, lhsT=lhsT, rhs=rhs, start=start, stop=stop, skip_group_check=True), reads, writes)

        def tr(out, in_, ident, reads, writes):
            P.op("pe", lambda e: e.transpose(out, in_, ident), reads, writes)

        def act(out, in_, func, reads, writes, bias=None, scale=None, accum=None):
            kw = {}
            if bias is not None:
                kw["bias"] = bias
            if scale is not None:
                kw["scale"] = scale
            if accum is not None:
                kw["accum_out"] = accum
            P.op("act", lambda e: e.activation(out=out, in_=in_, func=func, **kw), reads, writes)

        def tt(eng, out, in0, in1, op, reads, writes):
            P.op(eng, lambda e: e.tensor_tensor(out=out, in0=in0, in1=in1, op=op), reads, writes)

        def ts(eng, out, in0, s1, s2, op0, op1, reads, writes):
            if s2 is None:
                P.op(eng, lambda e: e.tensor_scalar(out=out, in0=in0, scalar1=s1, scalar2=None, op0=op0), reads, writes)
            else:
                P.op(eng, lambda e: e.tensor_scalar(out=out, in0=in0, scalar1=s1, scalar2=s2, op0=op0, op1=op1), reads, writes)

        def stt(eng, out, in0, scalar, in1, op0, op1, reads, writes):
            eng = "dve"
            P.op(eng, lambda e: e.scalar_tensor_tensor(out=out, in0=in0, scalar=scalar, in1=in1, op0=op0, op1=op1), reads, writes)

        def cp(eng, out, in_, reads, writes):
            P.op(eng, lambda e: e.tensor_copy(out=out, in_=in_), reads, writes)

        def mset(eng, out, val, writes):
            P.op(eng, lambda e: e.memset(out, val), (), writes)

        def tap(name, src, key, idx=None):
            if name in tap_d:
                dst = tap_d[name] if idx is None else tap_d[name][idx]
                P.dma(dst, src, reads=[key], writes=["tap_" + name], eng="pool")

        cb = A.bf(4480)
        ident = cb[:, 0:128]
        MC = cb[:, 128:1024]
        MW = cb[:, 1024:2432]
        HW32 = cb[:, 2432:4480]
        cf = A.f32(523)
        U1 = cf[:, 0:256]
        UT = cf[:, 0:128]
        ONES = cf[:, 128:256]
        WS = cf[:, 256:384]
        PB = cf[:, 384:392]
        FPt = cf[:, 392:395]
        identf = cf[:, 395:523]
        pv = A.f32(238)
        fnw = A.f32(8)
        xT = A.f32(8 * T)
        base_off = A.off

        P.dma(cb, cstb_d[:, :], writes=["cb"], eng="pool")
        P.dma(cf, cstf_d[:, :], writes=["cf"])
        P.dma(fnw, fnw_d[:, :], writes=["fnw"])

        NRM = {}

        def rms_stats():
            ps, pk = psum()
            for c in range(8):
                sqb, sqk = NRM["sq"][c % 2]
                act(sqb, xT[:, c * T:(c + 1) * T], AF.Square, ["xT"], [sqk])
                mm(ps[:, :], ONES, sqb, c == 0, c == 7, [sqk, "cf"], [pk])
            rb, rk = NRM["rstd"]
            act(rb, ps[:, :], AF.Sqrt, [pk], [rk], bias=EPS_AP, scale=1.0 / 1024.0)
            P.op("dve", lambda e: e.reciprocal(out=rb, in_=rb), [rk], [rk])
            return rb, rk

        def rmsnorm_T(woff, wt=None, wkey="pv"):
            wt = pv if wt is None else wt
            rb, rk = rms_stats()
            for c in range(8):
                stt("dve", hT[:, c * T:(c + 1) * T], xT[:, c * T:(c + 1) * T],
                    wt[:, woff + c:woff + c + 1], rb, ALU.mult, ALU.mult, ["xT", rk, wkey], ["hT"])

        epsb = A.f32(1)
        mset("dve", epsb, EPS, ["epsb"])
        EPS_AP = epsb[:, 0:1]
        base_off = A.off

        for l in range(NL):
            P.barrier()
            A.off = base_off
            w_in = A.bf(8 * NCOL)
            w_out = A.bf(8 * 1024)
            w1r = A.bf(32 * 128)
            w2k = A.bf(128)
            w2v = A.bf(64)
            posT = A.bf(32)
            rowb = A.f32(536)
            aneg = A.f32(8)
            cbias = A.f32(2)
            K1r = A.bf(S)
            Vs = A.bf((S // 128) * 65)
            Kw = A.bf(2 * T)
            Vw = A.bf(8 * 65)
            K3 = A.bf(16 + T)
            kcT = A.bf(512)
            vcc = A.bf(4 * 64)
            xcar = A.f32(8 * 3)
            cccar = A.f32(2 * 2)
            HTs = A.f32(512)
            HTb = A.bf(512)
            xbc = A.bf(8 * T)
            QT = A.bf(2 * T)
            zs = A.bf(4 * 512)
            dtt = A.f32(4 * 8)
            att = A.f32(4 * 8)
            gat = A.f32(4 * 12)
            ymT = A.bf(8 * T)
            xst = A.bf(4 * 512)
            Btk = A.bf(4 * 256)
            R_off = A.off
            hT = A.bf(8 * T)
            ropeA = A.f32(2 * T)
            ropeB = A.f32(2 * T)
            stage = A.f32(3 + T)
            acc = A.f32(T)
            acc2 = A.f32(T)
            scb = A.bf(2 * T)
            ccs = A.f32(2 * (2 + T))
            NRM.update(sq=[(acc, "acc"), (acc2, "acc2")], rstd=(stage[:, 0:T], "stage"))
            R_end = A.off
            A.off = R_off
            xdt = A.bf(2 * 512)
            xdte = A.bf(2 * 512)
            xdtf = A.f32(512)
            acs = A.f32(16)
            tot = A.f32(8)
            eacs = A.f32(16)
            dte = A.f32(16)
            cdec = A.f32(8)
            aU0_ = [A.f32(256) for _ in range(2)]
            aU1_ = [A.f32(128) for _ in range(2)]
            CBT = A.f32(2 * 2 * 256)
            E0_ = [A.f32(256) for _ in range(2)]
            E1_ = [A.f32(128) for _ in range(2)]
            MT0_ = [A.bf(256) for _ in range(2)]
            MT1_ = [A.bf(128) for _ in range(2)]
            ysb = A.f32(512)
            ysb2 = A.f32(512)
            ynb = A.bf(512)
            sm1 = A.f32(4)
            R_end = max(R_end, A.off)
            A.off = R_off
            Pc = [A.f32(512) for _ in range(2)]
            Psum_ = A.f32(520)
            Pcb_ = [A.bf(512) for _ in range(2)]
            PcT = A.bf(4 * 128)
            impb = A.f32(128)
            Vb = A.f32(128)
            Vb2 = A.f32(128)
            m8 = A.f32(16)
            selb = A.bf(128)
            selbT = A.bf(T)
            PT = [A.bf(T) for _ in range(3)]
            yq = A.f32(4 * 256)
            rs = A.f32(8)
            ynsa = A.bf(256)
            hidb = A.bf(2 * 32)
            hidv = A.bf(128)
            HW96 = A.bf(2048)
            R_end = max(R_end, A.off)
            A.off = R_end
            print('phaseA arena', A.off, 'base', base_off)

            for c in range(8):
                P.dma(w_in[:, c * NCOL:(c + 1) * NCOL], win_d[l, :, c, :], writes=["w_in"], eng="pool")
            for c in range(8):
                P.dma(w_out[:, c * 1024:(c + 1) * 1024], wout_d[l, :, c, :], writes=["w_out"], eng="pool")
            for j4 in range(4):
                P.dma(w1r[:, j4 * 1024:(j4 + 1) * 1024].rearrange("p (j h) -> p j h", h=128),
                      w1r_d[l, :, j4 * 8:(j4 + 1) * 8, :], writes=["w1r"], eng="pool")
            P.dma(w2k, w2k_d[l], writes=["w2k"], eng="pool")
            P.dma(w2v, w2v_d[l], writes=["w2v"], eng="pool")
            P.dma(posT, posT_d[l], writes=["posT"], eng="pool")
            P.dma(pv, pv_d[l], writes=["pv"])
            P.dma(rowb, rowv_d[l].partition_broadcast(128), writes=["rowb"])
            dtb = rowb[:, 0:8]
            Dsk = rowb[:, 16:24]
            nrmw = rowb[:, 24:536]
            act(aneg, rowb[:, 8:16], AF.Exp, ["rowb"], ["aneg"])
            ts("dve", aneg, aneg, -1.0, None, ALU.mult, None, ["aneg"], ["aneg"])
            for half in range(2):
                ps, pk = psum()
                r0 = half * 64
                for j in range(32):
                    mm(ps[:, 0:1], w1r[r0:r0 + 64, j * 128:(j + 1) * 128], posT[r0:r0 + 64, j:j + 1],
                       j == 0, j == 31, ["w1r", "posT"], [pk])
                cp("dve", cbias[:, half:half + 1], ps[:, 0:1], [pk], ["cbias"])
            mset("dve", HTs, 0.0, ["HTs"])
            mset("pool", HTb, 0.0, ["HTb"])
            mset("dve", xcar, 0.0, ["xcar"])
            mset("dve", cccar, 0.0, ["cccar"])
            mset("pool", K3, 0.0, ["K3"])
            mset("pool", Vs, 1.0, ["Vs"])
            mset("pool", Vw, 1.0, ["Vw"])
            mset("pool", kcT, 0.0, ["kcT"])
            mset("pool", vcc, 0.0, ["vcc"])
            mset("pool", Kw, 0.0, ["Kw"])

            src = xT_d if l == 0 else X1
            for ti in range(NT):
                t0 = ti * T
                for c in range(8):
                    P.dma(xT[:, c * T:(c + 1) * T], src[:, c, t0:t0 + T], writes=["xT"])
                P.dma(ropeA[:, 0:T], rope_d[0, :, t0:t0 + T], writes=["ropeA"])
                P.dma(ropeA[:, T:2 * T], rope_d[1, :, t0:t0 + T], writes=["ropeA"])
                P.dma(ropeB[:, 0:T], rope_d[2, :, t0:t0 + T], writes=["ropeB"])
                P.dma(ropeB[:, T:2 * T], rope_d[3, :, t0:t0 + T], writes=["ropeB"])
                rmsnorm_T(0)
                if l == 0 and ti == 0:
                    tap("hT", hT, "hT")

                def proj_fm(ch):
                    ps, pk = psum()
                    for c in range(8):
                        mm(ps[:, :], w_in[:, c * NCOL + ch * 128: c * NCOL + (ch + 1) * 128], hT[:, c * T:(c + 1) * T],
                           c == 0, c == 7, ["w_in", "hT"], [pk])
                    return ps, pk

                for ch in range(8):
                    ps, pk = proj_fm(ch)
                    cp("pool", stage[:, 0:3], xcar[:, ch * 3:ch * 3 + 3], ["xcar"], ["stage"])
                    act(stage[:, 3:3 + T], ps[:, :], AF.Identity, [pk, "stage"], ["stage"])
                    cw = 16 + ch * 4
                    act(acc, ps[:, :], AF.Identity, [pk, "pv"], ["acc"], bias=pv[:, 48 + ch:49 + ch], scale=pv[:, cw + 3:cw + 4])
                    for j in range(3):
                        stt("dve", acc, stage[:, j:j + T], pv[:, cw + j:cw + j + 1], acc, ALU.mult, ALU.add,
                            ["stage", "acc", "pv"], ["acc"])
                    cp("pool", xcar[:, ch * 3:ch * 3 + 3], stage[:, T:T + 3], ["stage"], ["xcar"])
                    act(xbc[:, ch * T:(ch + 1) * T], acc, AF.Silu, ["acc"], ["xbc"])
                if l == 0 and ti == 0:
                    tap("xbc", xbc, "xbc")
                for c2 in range(2):
                    ps, pk = proj_fm(8 + c2)
                    act(scb[:, c2 * T:(c2 + 1) * T], ps[:, :], AF.Identity, [pk], ["scb"])
                for c2 in range(2):
                    psc, pkc = proj_fm(10 + c2)
                    psh, pkh = proj_fm(12 + c2)
                    o = c2 * (2 + T)
                    act(acc2, psc[:, :], AF.Identity, [pkc], ["acc2"])
                    cp("pool", ccs[:, o:o + 2], cccar[:, c2 * 2:c2 * 2 + 2], ["cccar"], ["ccs"])
                    tt("dve", ccs[:, o + 2:o + 2 + T], psh[:, :], acc2, ALU.mult, [pkh, "acc2", "ccs"], ["ccs"])
                    cw = 56 + c2 * 3
                    ts("dve", acc, ccs[:, o + 2:o + 2 + T], pv[:, cw + 2:cw + 3], None, ALU.mult, None, ["ccs", "pv"], ["acc"])
                    for j in range(2):
                        stt("dve", acc, ccs[:, o + j:o + j + T], pv[:, cw + j:cw + j + 1], acc, ALU.mult, ALU.add,
                            ["ccs", "acc", "pv"], ["acc"])
                    cp("pool", cccar[:, c2 * 2:c2 * 2 + 2], ccs[:, o + T:o + T + 2], ["ccs"], ["cccar"])
                    tt("dve", ymT[:, (4 + c2) * T:(5 + c2) * T], acc, scb[:, c2 * T:(c2 + 1) * T], ALU.mult,
                       ["acc", "scb"], ["ymT"])

                def roped(ch, chp, tab, tkey, out, okey):
                    ps, pk = proj_fm(ch)
                    ps2, pk2 = proj_fm(chp)
                    tt("dve", acc, ps[:, :], tab[:, 0:T], ALU.mult, [pk, tkey], ["acc"])
                    tt("dve", acc2, ps2[:, :], tab[:, T:2 * T], ALU.mult, [pk2, tkey], ["acc2"])
                    tt("pool", out, acc, acc2, ALU.add, ["acc", "acc2"], [okey])
                roped(14, 16, ropeA, "ropeA", QT[:, 0:T], "QT")
                roped(15, 17, ropeA, "ropeA", QT[:, T:2 * T], "QT")
                roped(18, 19, ropeA, "ropeA", K1r[:, t0:t0 + T], "K1r")
                ws = (ti % 2) * T
                roped(20, 21, ropeA, "ropeA", Kw[:, ws:ws + T], "Kw")
                cp("pool", K3[:, 0:16], K3[:, T:T + 16], ["K3"], ["K3"])
                roped(22, 23, ropeB, "ropeB", K3[:, 16:16 + T], "K3")
                for q4 in range(4):
                    ps, pk = psum()
                    for c in range(8):
                        mm(ps[:, :], hT[:, c * T + q4 * 128: c * T + (q4 + 1) * 128], w_in[:, c * NCOL + 3072: c * NCOL + 3584],
                           c == 0, c == 7, ["hT", "w_in"], [pk])
                    act(zs[:, q4 * 512:(q4 + 1) * 512], ps[:, :], AF.Silu, [pk], ["zs"])
                    ps, pk = psum()
                    for c in range(8):
                        mm(ps[:, 0:148], hT[:, c * T + q4 * 128: c * T + (q4 + 1) * 128], w_in[:, c * NCOL + 3584: c * NCOL + 3732],
                           c == 0, c == 7, ["hT", "w_in"], [pk])
                    tt("dve", dtt[:, q4 * 8:(q4 + 1) * 8], ps[:, 0:8], dtb, ALU.add, [pk, "rowb"], ["dtt"])
                    gt = ti * 4 + q4
                    cp("dve", Vs[:, gt * 65:gt * 65 + 64], ps[:, 8:72], [pk], ["Vs"])
                    sl = (gt % 8) * 65
                    cp("dve", Vw[:, sl:sl + 64], ps[:, 72:136], [pk], ["Vw"])
                    act(gat[:, q4 * 12:(q4 + 1) * 12], ps[:, 136:148], AF.Sigmoid, [pk], ["gat"])
                act(dtt, dtt, AF.Exp, ["dtt"], ["dtt"])
                act(dtt, dtt, AF.Ln, ["dtt"], ["dtt"], bias=1.0, scale=1.0)
                for q4 in range(4):
                    tt("dve", att[:, q4 * 8:(q4 + 1) * 8], dtt[:, q4 * 8:(q4 + 1) * 8], aneg, ALU.mult, ["dtt", "aneg"], ["att"])
                if l == 0 and ti == 0:
                    tap("dtt", dtt, "dtt")
                    tap("QT", QT, "QT")

                psbf = [p[:, :].bitcast(BF16) for p in psb]
                for q4 in range(4):
                    i = psrot[0] % 6
                    psrot[0] += 1
                    pk = "ps%d" % i
                    for c in range(6):
                        tr(psbf[i][:, c * 128:(c + 1) * 128], xbc[:, c * T + q4 * 128: c * T + (q4 + 1) * 128], ident,
                           ["xbc", "cb"], [pk])
                    cp("dve", xst[:, q4 * 512:(q4 + 1) * 512], psbf[i][:, 0:512], [pk], ["xst"])
                    act(Btk[:, q4 * 256:(q4 + 1) * 256], psbf[i][:, 512:768], AF.Identity, [pk], ["Btk"])

                P.barrier()
                for c2 in range(2):
                    tt0, tt1 = 2 * c2, 2 * c2 + 1
                    ck = c2 * 256
                    ps, pk = psum()
                    mm(ps[:, 0:8], UT, att[:, tt0 * 8:tt0 * 8 + 8], True, True, ["cf", "att"], [pk])
                    mm(ps[:, 8:16], ONES, att[:, tt0 * 8:tt0 * 8 + 8], True, False, ["cf", "att"], [pk])
                    mm(ps[:, 8:16], UT, att[:, tt1 * 8:tt1 * 8 + 8], False, True, ["cf", "att"], [pk])
                    mm(ps[:, 16:24], ONES, att[:, tt0 * 8:tt0 * 8 + 8], True, False, ["cf", "att"], [pk])
                    mm(ps[:, 16:24], ONES, att[:, tt1 * 8:tt1 * 8 + 8], False, True, ["cf", "att"], [pk])
                    cp("dve", acs, ps[:, 0:16], [pk], ["acs"])
                    cp("dve", tot, ps[:, 16:24], [pk], ["tot"])
                    if l == 0 and ti == 0 and c2 == 0:
                        tap("acs", acs, "acs")
                    act(eacs, acs, AF.Exp, ["acs"], ["eacs"])
                    for lt in range(2):
                        tt("dve", dte[:, lt * 8:lt * 8 + 8], tot, acs[:, lt * 8:lt * 8 + 8], ALU.subtract, ["tot", "acs"], ["dte"])
                    act(dte, dte, AF.Exp, ["dte"], ["dte"])
                    act(cdec, tot, AF.Exp, ["tot"], ["cdec"])
                    for lt in range(2):
                        tk = 2 * c2 + lt
                        x3 = xst[:, tk * 512:(tk + 1) * 512].rearrange("p (h d) -> p h d", d=64)
                        d3 = dtt[:, tk * 8:tk * 8 + 8].unsqueeze(2).to_broadcast([128, 8, 64])
                        tt("dve", xdtf[:, :].rearrange("p (h d) -> p h d", d=64), x3, d3, ALU.mult, ["xst", "dtt"], ["xdtf"])
                        cp("pool", xdt[:, lt * 512:(lt + 1) * 512], xdtf, ["xdtf"], ["xdt"])
                        e3 = dte[:, lt * 8:lt * 8 + 8].unsqueeze(2).to_broadcast([128, 8, 64])
                        tt("dve", xdte[:, lt * 512:(lt + 1) * 512].rearrange("p (h d) -> p h d", d=64),
                           xdtf[:, :].rearrange("p (h d) -> p h d", d=64), e3, ALU.mult, ["xdtf", "dte"], ["xdte"])
                    for g in range(2):
                        for s_t in range(2):
                            ps, pk = psum()
                            mm(ps[:, 0:256], xbc[:, (4 + g) * T + ck + s_t * 128:(4 + g) * T + ck + (s_t + 1) * 128],
                               xbc[:, (6 + g) * T + ck:(6 + g) * T + ck + 256], True, True, ["xbc"], [pk])
                            o = (g * 2 + s_t) * 256
                            act(CBT[:, o:o + 256], ps[:, 0:256], AF.Identity, [pk], ["CBT"])
                    def ssd_front(h):
                        g = h // 4
                        pb = h % 2
                        aU0, aU1, E0, E1, MT0, MT1 = aU0_[pb], aU1_[pb], E0_[pb], E1_[pb], MT0_[pb], MT1_[pb]
                        k0, k1, ke0, ke1, km0, km1 = ("aU0%d" % pb, "aU1%d" % pb, "E0%d" % pb, "E1%d" % pb, "MT0%d" % pb, "MT1%d" % pb)
                        ts("dve", aU0, U1, att[:, tt0 * 8 + h:tt0 * 8 + h + 1], None, ALU.mult, None, ["cf", "att"], [k0])
                        ts("pool", aU1, UT, att[:, tt1 * 8 + h:tt1 * 8 + h + 1], None, ALU.mult, None, ["cf", "att"], [k1])
                        ps0, pk0 = psum()
                        mm(ps0[:, 0:256], WS, aU0, True, False, ["cf", k0], [pk0])
                        mm(ps0[:, 128:256], ONES, aU1, False, False, ["cf", k1], [pk0])
                        mm(ps0[:, 0:256], ident, MC[:, 384:640], False, True, ["cb"], [pk0])
                        mm(ps0[:, 256:384], WS, aU1, True, False, ["cf", k1], [pk0])
                        mm(ps0[:, 256:384], ident, MC[:, 384:512], False, True, ["cb"], [pk0])
                        act(E0, ps0[:, 0:256], AF.Exp, [pk0], [ke0])
                        act(E1, ps0[:, 256:384], AF.Exp, [pk0], [ke1])
                        o = g * 512
                        tt("dve", MT0, E0, CBT[:, o:o + 256], ALU.mult, [ke0, "CBT"], [km0])
                        tt("pool", MT1, E1, CBT[:, o + 256 + 128:o + 512], ALU.mult, [ke1, "CBT"], [km1])

                    def ssd_back(h):
                        pb = h % 2
                        MT0, MT1 = MT0_[pb], MT1_[pb]
                        km0, km1 = "MT0%d" % pb, "MT1%d" % pb
                        hs = slice(h * 64, (h + 1) * 64)
                        mm(psb[6][:, hs], MT0[:, 0:128], xdt[:, h * 64:(h + 1) * 64], True, True, [km0, "xdt"], ["ps6"])
                        mm(psb[7][:, hs], MT0[:, 128:256], xdt[:, h * 64:(h + 1) * 64], True, False, [km0, "xdt"], ["ps7"])
                        mm(psb[7][:, hs], MT1, xdt[:, 512 + h * 64:512 + (h + 1) * 64], False, True, [km1, "xdt"], ["ps7"])
                    ssd_front(0)
                    for h in range(8):
                        if h + 1 < 8:
                            ssd_front(h + 1)
                        ssd_back(h)
                    for lt in range(2):
                        tk = 2 * c2 + lt
                        ps, pk = psum()
                        for g in range(2):
                            mm(ps[:, g * 256:(g + 1) * 256], xbc[:, (6 + g) * T + ck + lt * 128:(6 + g) * T + ck + (lt + 1) * 128],
                               HTb[:, g * 256:(g + 1) * 256], True, True, ["xbc", "HTb"], [pk])
                        e3 = eacs[:, lt * 8:lt * 8 + 8].unsqueeze(2).to_broadcast([128, 8, 64])
                        tt("dve", ysb[:, :].rearrange("p (h d) -> p h d", d=64), ps[:, :].rearrange("p (h d) -> p h d", d=64), e3,
                           ALU.mult, [pk, "eacs"], ["ysb"])
                        tt("dve", ysb, ysb, psb[6 + lt][:, :], ALU.add, ["ysb", "ps%d" % (6 + lt)], ["ysb"])
                        if l == 0 and ti == 0 and c2 == 0 and lt == 0:
                            tap("ydiag", ysb, "ysb")
                        D3 = Dsk.unsqueeze(2).to_broadcast([128, 8, 64])
                        tt("pool", ysb2[:, :].rearrange("p (h d) -> p h d", d=64),
                           xst[:, tk * 512:(tk + 1) * 512].rearrange("p (h d) -> p h d", d=64), D3, ALU.mult, ["xst", "rowb"], ["ysb2"])
                        tt("dve", ysb, ysb, ysb2, ALU.add, ["ysb", "ysb2"], ["ysb"])
                        tt("dve", ysb, ysb, zs[:, tk * 512:(tk + 1) * 512], ALU.mult, ["ysb", "zs"], ["ysb"])
                        if l == 0 and ti == 0 and c2 == 0 and lt == 0:
                            tap("yssd", ysb, "ysb")
                        mset("dve", sm1[:, 0:1], 0.0, ["sm1"])
                        act(ysb2, ysb, AF.Square, ["ysb", "sm1"], ["ysb2", "sm1"], accum=sm1[:, 0:1])
                        act(sm1[:, 1:2], sm1[:, 0:1], AF.Sqrt, ["sm1"], ["sm1"], bias=EPS_AP, scale=1.0 / 512.0)
                        P.op("dve", lambda e: e.reciprocal(out=sm1[:, 2:3], in_=sm1[:, 1:2]), ["sm1"], ["sm1"])
                        stt("dve", ynb, ysb, sm1[:, 2:3], nrmw, ALU.mult, ALU.mult, ["ysb", "sm1", "rowb"], ["ynb"])
                        i = psrot[0] % 6
                        psrot[0] += 1
                        pk2 = "ps%d" % i
                        for c in range(4):
                            tr(psbf[i][:, c * 128:(c + 1) * 128], ynb[:, c * 128:(c + 1) * 128], ident, ["ynb", "cb"], [pk2])
                        for c in range(4):
                            o = c * T + ck + lt * 128
                            if c % 2:
                                cp("dve", ymT[:, o:o + 128], psbf[i][:, c * 128:(c + 1) * 128], [pk2], ["ymT"])
                            else:
                                act(ymT[:, o:o + 128], psbf[i][:, c * 128:(c + 1) * 128], AF.Identity, [pk2], ["ymT"])
                    for g in range(2):
                        ps, pk = psum()
                        for lt in range(2):
                            tk = 2 * c2 + lt
                            mm(ps[:, 0:256], Btk[:, tk * 256 + g * 128: tk * 256 + (g + 1) * 128],
                               xdte[:, lt * 512 + g * 256: lt * 512 + (g + 1) * 256], lt == 0, lt == 1, ["Btk", "xdte"], [pk])
                        for k in range(4):
                            h = g * 4 + k
                            stt("dve", HTs[:, h * 64:(h + 1) * 64], HTs[:, h * 64:(h + 1) * 64], cdec[:, h:h + 1],
                                ps[:, k * 64:(k + 1) * 64], ALU.mult, ALU.add, ["HTs", "cdec", pk], ["HTs"])
                    cp("pool", HTb, HTs, ["HTs"], ["HTb"])

                P.barrier()
                for half in range(2):
                    r0 = half * 64
                    ps, pk = psum()
                    for j in range(32):
                        mm(ps[:, 0:32], w1r[r0:r0 + 64, j * 128:(j + 1) * 128], K3[r0:r0 + 64, j:j + 16 * 31 + 1:16],
                           j == 0, j == 31, ["w1r", "K3"], [pk])
                    if half == 0:
                        act(hidb[:, 0:32], ps[:, 0:32], AF.Gelu_apprx_tanh, [pk, "cbias"], ["hidb"],
                            bias=cbias[:, 0:1], scale=1.0)
                    else:
                        mset("pool", hidv, 0.0, ["hidv"])
                        pa_ = 32 * (ti % 4)
                        act(hidv[:, pa_:pa_ + 32], ps[:, 0:32], AF.Gelu_apprx_tanh, [pk, "cbias", "hidv"], ["hidv"],
                            bias=cbias[:, 1:2], scale=1.0)
                ps, pk = psum()
                mm(ps[:, 0:32], w2k, hidb[:, 0:32], True, True, ["w2k", "hidb"], [pk])
                cp("dve", kcT[:, 32 * ti:32 * ti + 32], ps[:, 0:32], [pk], ["kcT"])
                pa = 32 * (ti % 4)
                ps, pk = psum()
                mm(ps[:, 0:64], hidv[:, :], w2v, True, True, ["w2v", "hidv"], [pk])
                o = (ti // 4) * 64
                tt("dve", vcc[:, o:o + 64], vcc[:, o:o + 64], ps[:, 0:64], ALU.add, [pk, "vcc"], ["vcc"])
                if ti == 0:
                    mset("dve", kcT[:, 0:1], 0.0, ["kcT"])
                if 4 * ti + 3 >= 48:
                    mset("pool", HW96[0:96, :], 0.0, ["HW96"])
                    cp("pool", HW96[96:128, :], HW32[96:128, :], ["cb", "HW96"], ["HW96"])

                def cmp_front(qb, h):
                    qbg = ti * 4 + qb
                    Wc = 8 * (qbg + 1)
                    hp = (h % 2) * 64
                    qch = h // 2
                    ps, pk = psum()
                    mm(ps[:, 0:Wc], QT[hp:hp + 64, qch * T + qb * 128: qch * T + (qb + 1) * 128], kcT[hp:hp + 64, 0:Wc],
                       True, True, ["QT", "kcT"], [pk])
                    pc = Pc[h % 2]
                    pck = "Pc%d" % (h % 2)
                    act(pc[:, 0:Wc], ps[:, 0:Wc], AF.Exp, [pk], [pck], scale=0.125)
                    tt("dve", pc[:, Wc - 8:Wc], pc[:, Wc - 8:Wc], PB, ALU.mult, [pck, "cf"], [pck])
                    mset("dve", pc[:, 0:1], 0.0, [pck])
                    P.op("dve", lambda e: e.reduce_sum(out=rs[:, h:h + 1], in_=pc[:, 0:Wc], axis=AX.X), [pck], ["rs"])
                    ts("dve", rs[:, 4 + h:5 + h], rs[:, h:h + 1], 1e-20, None, ALU.max, None, ["rs"], ["rs"])
                    P.op("dve", lambda e: e.reciprocal(out=rs[:, 4 + h:5 + h], in_=rs[:, 4 + h:5 + h]), ["rs"], ["rs"])
                    ts("dve", pc[:, 0:Wc], pc[:, 0:Wc], rs[:, 4 + h:5 + h], None, ALU.mult, None, [pck, "rs"], [pck])
                    if h == 0:
                        mset("pool", Psum_, 0.0, ["Psum"])
                        cp("pool", Psum_[:, 0:Wc], pc[:, 0:Wc], [pck, "Psum"], ["Psum"])
                    else:
                        tt("pool", Psum_[:, 0:Wc], Psum_[:, 0:Wc], pc[:, 0:Wc], ALU.add, [pck, "Psum"], ["Psum"])
                    ib = (qb * 4 + h) % 2
                    cp("pool", Pcb_[ib][:, 0:Wc], pc[:, 0:Wc], [pck], ["Pcb%d" % ib])

                def cmp_back(qb, h):
                    qbg = ti * 4 + qb
                    Wc = 8 * (qbg + 1)
                    nch = (Wc + 127) // 128
                    ib = (qb * 4 + h) % 2
                    Pcb = Pcb_[ib]
                    i = psrot[0] % 6
                    psrot[0] += 1
                    pk2 = "ps%d" % i
                    for c in range(nch):
                        w = min(128, Wc - c * 128)
                        tr(psbf[i][0:w, c * 128:(c + 1) * 128], Pcb[:, c * 128:c * 128 + w], ident, ["Pcb%d" % ib, "cb"], [pk2])
                    for c in range(nch):
                        w = min(128, Wc - c * 128)
                        cp("dve", PcT[0:w, c * 128:(c + 1) * 128], psbf[i][0:w, c * 128:(c + 1) * 128], [pk2], ["PcT"])
                    ps, pk = psum()
                    for c in range(nch):
                        w = min(128, Wc - c * 128)
                        mm(ps[:, 0:64], PcT[0:w, c * 128:(c + 1) * 128], vcc[0:w, c * 64:(c + 1) * 64], c == 0, c == nch - 1,
                           ["PcT", "vcc"], [pk])
                    ts("dve", yq[:, qb * 256 + h * 64:qb * 256 + (h + 1) * 64], ps[:, 0:64],
                       gat[:, qb * 12 + h * 3:qb * 12 + h * 3 + 1], None, ALU.mult, None, [pk, "gat"], ["yq"])

                def cmp_topk(qb):
                    qbg = ti * 4 + qb
                    qs = slice(qb * 128, (qb + 1) * 128)
                    P.op("dve", lambda e: e.tensor_reduce(out=impb, in_=Psum_[:, 0:512].rearrange("p (j k) -> p j k", k=4),
                                                          axis=AX.X, op=ALU.add), ["Psum"], ["impb"])
                    tt("dve", impb, impb, Psum_[:, 4:516:4], ALU.add, ["impb", "Psum"], ["impb"])
                    Wb = 2 * qbg + 2
                    mset("pool", Vb, -1e30, ["Vb"])
                    cp("dve", Vb[:, 0:Wb], impb[:, 0:Wb], ["impb", "Vb"], ["Vb"])
                    j0 = 2 * qbg
                    if j0 - 1 >= 0:
                        ts("dve", Vb[:, j0 - 1:j0], Vb[:, j0 - 1:j0], FPt[:, 0:1], FPt[:, 1:2], ALU.mult, ALU.add, ["Vb", "cf"], ["Vb"])
                    mset("dve", Vb[:, j0:j0 + 1], 1e9, ["Vb"])
                    cp("dve", Vb[:, j0 + 1:j0 + 2], FPt[:, 2:3], ["cf", "Vb"], ["Vb"])
                    mset("dve", Vb[:, 0:1], 1e9, ["Vb"])
                    if Wb >= 18:
                        P.op("dve", lambda e: e.max(out=m8[:, 0:8], in_=Vb[:, :]), ["Vb"], ["m8"])
                        P.op("dve", lambda e: e.match_replace(out=Vb2[:, :], in_to_replace=m8[:, 0:8], in_values=Vb[:, :], imm_value=-3e38),
                             ["Vb", "m8"], ["Vb2"])
                        P.op("dve", lambda e: e.max(out=m8[:, 8:16], in_=Vb2[:, :]), ["Vb2"], ["m8"])
                        ts("dve", Vb2, Vb, m8[:, 15:16], None, ALU.is_ge, None, ["Vb", "m8"], ["Vb2"])
                    else:
                        ts("dve", Vb2, Vb, -1e29, None, ALU.is_ge, None, ["Vb"], ["Vb2"])
                    ts("dve", selb, Vb2, -1.0, BIG, ALU.add, ALU.mult, ["Vb2"], ["selb"])
                    i = psrot[0] % 6
                    psrot[0] += 1
                    pk2 = "ps%d" % i
                    tr(psbf[i][:, 0:128], selb, ident, ["selb", "cb"], [pk2])
                    cp("dve", selbT[:, qs], psbf[i][:, 0:128], [pk2], ["selbT"])

                its = [(qb, h) for qb in range(4) for h in range(4)]
                cmp_front(*its[0])
                for n_, (qb, h) in enumerate(its):
                    if n_ + 1 < len(its):
                        if its[n_ + 1][1] == 0:
                            cmp_topk(qb)
                        cmp_front(*its[n_ + 1])
                    cmp_back(qb, h)
                cmp_topk(3)
                if l == 0 and ti == 0:
                    tap("yq_c", yq, "yq")
                    tap("kcT", kcT, "kcT")
                    tap("selbT", selbT, "selbT")
                items = []
                for br in range(2):
                    for h in range(4):
                        if br == 0:
                            kts = list(range(0, 4 * ti + 4))
                        else:
                            kts = [k for k in range(4 * ti - 4, 4 * ti + 4) if k >= 0]
                        for n, kt in enumerate(kts):
                            items.append((br, h, n, kt, len(kts)))

                def sw_front(idx):
                    br, h, n, kt, nk = items[idx]
                    hp = (h % 2) * 64
                    qch = h // 2
                    qT_h = QT[hp:hp + 64, qch * T:(qch + 1) * T]
                    ps, pk = psum()
                    if br == 0:
                        mm(ps[:, :], K1r[hp:hp + 64, kt * 128:(kt + 1) * 128], qT_h, True, False, ["K1r", "QT"], [pk])
                        a32 = (2 * kt) // 32
                        kk = kt % 16
                        d = kt - 4 * ti
                        if a32 < 3:
                            mm(ps[:, :], HW32[32 * a32:32 * a32 + 32, kk * 128:(kk + 1) * 128], selbT[32 * a32:32 * a32 + 32, :],
                               False, d < 0, ["cb", "selbT"], [pk])
                        else:
                            mm(ps[:, :], HW96[:, kk * 128:(kk + 1) * 128], selbT[:, :],
                               False, d < 0, ["HW96", "selbT"], [pk])
                        if d >= 0:
                            mm(ps[:, :], ident, MC[:, 384 - 128 * d:384 - 128 * d + 512], False, True, ["cb"], [pk])
                    else:
                        slot = ((kt // 4) % 2) * T + (kt % 4) * 128
                        mm(ps[:, :], Kw[hp:hp + 64, slot:slot + 128], qT_h, True, False, ["Kw", "QT"], [pk])
                        j = kt - (4 * ti - 4)
                        mm(ps[:, :], ident, MW[:, 896 - 128 * j:896 - 128 * j + 512], False, True, ["cb"], [pk])
                    pt = PT[idx % 3]
                    ptk = "PT%d" % (idx % 3)
                    act(pt, ps[:, :], AF.Exp, [pk], [ptk], scale=0.125)

                def sw_back(idx):
                    br, h, n, kt, nk = items[idx]
                    pt = PT[idx % 3]
                    ptk = "PT%d" % (idx % 3)
                    ob = 6 + ((br * 4 + h) % 2)
                    oacc = psb[ob]
                    oak = "ps%d" % ob
                    for qb in range(4):
                        if br == 0:
                            vv = Vs[:, kt * 65:(kt + 1) * 65]
                            vk = "Vs"
                        else:
                            vv = Vw[:, (kt % 8) * 65:(kt % 8 + 1) * 65]
                            vk = "Vw"
                        mmg(oacc[:, qb * 128:qb * 128 + 65], pt[:, qb * 128:(qb + 1) * 128], vv, n == 0 and qb == 0,
                            n == nk - 1, [ptk, vk], [oak])
                    if n == nk - 1:
                        for qb in range(4):
                            ts("dve", rs[:, 0:1], oacc[:, qb * 128 + 64:qb * 128 + 65], 1e-30, None, ALU.max, None, [oak], ["rs"])
                            P.op("dve", lambda e: e.reciprocal(out=rs[:, 1:2], in_=rs[:, 0:1]), ["rs"], ["rs"])
                            gi = qb * 12 + h * 3 + 1 + br
                            tt("dve", rs[:, 1:2], rs[:, 1:2], gat[:, gi:gi + 1], ALU.mult, ["rs", "gat"], ["rs"])
                            yv = yq[:, qb * 256 + h * 64:qb * 256 + (h + 1) * 64]
                            stt("dve", yv, oacc[:, qb * 128:qb * 128 + 64], rs[:, 1:2], yv, ALU.mult, ALU.add, [oak, "rs", "yq"], ["yq"])
                sw_front(0)
                for idx in range(len(items)):
                    if idx + 1 < len(items):
                        sw_front(idx + 1)
                    sw_back(idx)
                if l == 0 and ti == 0:
                    tap("yq", yq, "yq")
                for qb in range(4):
                    cp("pool", ynsa, yq[:, qb * 256:(qb + 1) * 256], ["yq"], ["ynsa"])
                    i = psrot[0] % 6
                    psrot[0] += 1
                    pk2 = "ps%d" % i
                    for c in range(2):
                        tr(psbf[i][:, c * 128:(c + 1) * 128], ynsa[:, c * 128:(c + 1) * 128], ident, ["ynsa", "cb"], [pk2])
                    for c in range(2):
                        o = (6 + c) * T + qb * 128
                        cp("dve", ymT[:, o:o + 128], psbf[i][:, c * 128:(c + 1) * 128], [pk2], ["ymT"])
                if l == 0 and ti == 0:
                    tap("ymT", ymT, "ymT")

                P.barrier()
                for co in range(8):
                    ps, pk = psum()
                    for c in range(8):
                        mm(ps[:, :], w_out[:, c * 1024 + co * 128: c * 1024 + (co + 1) * 128], ymT[:, c * T:(c + 1) * T],
                           c == 0, c == 7, ["w_out", "ymT"], [pk])
                    tt("dve", xT[:, co * T:(co + 1) * T], xT[:, co * T:(co + 1) * T], ps[:, :], ALU.add, ["xT", pk], ["xT"])
                for c in range(8):
                    P.dma(X2[:, c, t0:t0 + T], xT[:, c * T:(c + 1) * T], reads=["xT"], writes=["X2"])

            P.barrier()
            A.off = base_off
            hT = A.bf(8 * T)
            w_up = A.bf(8 * 5504)
            w_dn = A.bf(22 * 1024)
            ucar = A.f32(44 * 2)
            ust = [[A.f32(2 + T) for _ in range(2)] for _ in range(2)]
            fa = [[A.f32(T) for _ in range(2)] for _ in range(2)]
            NRM.update(sq=[(fa[0][0], "fa00"), (fa[1][0], "fa10")], rstd=(ust[0][0][:, 0:T], "ust00"))
            aT = A.bf(22 * T)
            print('phaseB arena', A.off)
            for c in range(8):
                P.dma(w_up[:, c * 5504:(c + 1) * 5504], wup_d[l, :, c, :], writes=["w_up"], eng="pool")
            for c in range(3):
                P.dma(w_dn[:, c * 7168:(c + 1) * 7168].rearrange("p (k n) -> p k n", n=1024),
                      wdn_d[l, c * 896:(c + 1) * 896, :].rearrange("(k p) n -> p k n", p=128), writes=["w_dn"], eng="pool")
            P.dma(w_dn[0:64, 21 * 1024:22 * 1024], wdn_d[l, 2688:2752, :], writes=["w_dn"], eng="pool")
            mset("dve", ucar, 0.0, ["ucar"])
            last = (l == NL - 1)
            for ti in range(NT):
                t0 = ti * T
                for c in range(8):
                    P.dma(xT[:, c * T:(c + 1) * T], X2[:, c, t0:t0 + T], reads=["X2"], writes=["xT"])
                rmsnorm_T(8)
                for j in range(22):
                    np_ = 128 if j < 21 else 64
                    res = []
                    for half in range(2):
                        col = half * D_FF + j * 128
                        ps, pk = psum()
                        for c in range(8):
                            mm(ps[0:np_, :], w_up[:, c * 5504 + col: c * 5504 + col + np_], hT[:, c * T:(c + 1) * T],
                               c == 0, c == 7, ["w_up", "hT"], [pk])
                        par = j % 2
                        u = ust[half][par]
                        uk = "ust%d%d" % (half, par)
                        ci = half * 22 + j
                        cp("pool", u[0:np_, 0:2], ucar[0:np_, ci * 2:ci * 2 + 2], ["ucar"], [uk])
                        act(u[0:np_, 2:2 + T], ps[0:np_, :], AF.Identity, [pk, uk], [uk])
                        cw = 62 + half * 66 + j * 3
                        bo = 194 + half * 22 + j
                        f = fa[half][par]
                        fk = "fa%d%d" % (half, par)
                        act(f[0:np_, :], ps[0:np_, :], AF.Identity, [pk, "pv"], [fk], bias=pv[0:np_, bo:bo + 1], scale=pv[0:np_, cw + 2:cw + 3])
                        for k in range(2):
                            stt("dve" if half == 0 else "pool", f[0:np_, :], u[0:np_, k:k + T], pv[0:np_, cw + k:cw + k + 1], f[0:np_, :],
                                ALU.mult, ALU.add, [uk, fk, "pv"], [fk])
                        cp("pool", ucar[0:np_, ci * 2:ci * 2 + 2], u[0:np_, T:T + 2], [uk], ["ucar"])
                    par = j % 2
                    act(fa[0][par][0:np_, :], fa[0][par][0:np_, :], AF.Silu, ["fa0%d" % par], ["fa0%d" % par])
                    tt("pool", aT[0:np_, j * T:(j + 1) * T], fa[0][par][0:np_, :], fa[1][par][0:np_, :], ALU.mult,
                       ["fa0%d" % par, "fa1%d" % par], ["aT"])
                for co in range(8):
                    ps, pk = psum()
                    for j in range(22):
                        np_ = 128 if j < 21 else 64
                        mm(ps[:, :], w_dn[0:np_, j * 1024 + co * 128: j * 1024 + (co + 1) * 128], aT[0:np_, j * T:(j + 1) * T],
                           j == 0, j == 21, ["w_dn", "aT"], [pk])
                    tt("dve", xT[:, co * T:(co + 1) * T], xT[:, co * T:(co + 1) * T], ps[:, :], ALU.add, ["xT", pk], ["xT"])
                if not last:
                    for c in range(8):
                        P.dma(X1[:, c, t0:t0 + T], xT[:, c * T:(c + 1) * T], reads=["xT"], writes=["X1"])
                else:
                    rb, rk = rms_stats()
                    for c in range(8):
                        o_ = fa[c % 2][1]
                        ok_ = "fa%d1" % (c % 2)
                        stt("dve", o_, xT[:, c * T:(c + 1) * T], fnw[:, c:c + 1], rb, ALU.mult, ALU.mult, ["xT", rk, "fnw"], [ok_])
                        P.dma(outT_d[:, c, t0:t0 + T], o_, reads=[ok_], writes=["outT"])
        P.emit(final_keys=["outT"] + ["tap_" + n for n in tap_d])
    return nc


IN_OFF = dict(z=0, xbc=512, dt=1536, sc_b=1544, sc_c=1800, sc_h=2056, q=2312, k_c=2568, v_c=2632,
              k_s=2696, v_s=2760, k_w=2824, v_w=2888, g=2952)


def _cols():
    r = np.arange
    perm = (r(64) + 32) % 64
    o = IN_OFF
    cols = [r(o["xbc"], o["xbc"] + 1024), r(o["sc_b"], o["sc_b"] + 768), r(o["q"], o["q"] + 256)]
    cols.append(np.concatenate([o["q"] + h * 64 + perm for h in range(4)]))
    for k in ("k_s", "k_w"):
        cols += [r(o[k], o[k] + 64), r(o[k], o[k] + 64), o[k] + perm, o[k] + perm]
    cols += [r(o["k_c"], o["k_c"] + 64), r(o["v_c"], o["v_c"] + 64), o["k_c"] + perm, r(o["v_c"], o["v_c"] + 64)]
    cols += [r(o["z"], o["z"] + 512), r(o["dt"], o["dt"] + 8), r(o["v_s"], o["v_s"] + 64), r(o["v_w"], o["v_w"] + 64),
             r(o["g"], o["g"] + 12)]
    c = np.concatenate(cols)
    assert c.shape[0] == NCOL
    return c


def _fm(w):
    return np.ascontiguousarray(w.reshape(8, 128, -1).transpose(1, 0, 2))


def _pc(v, n):
    return v.reshape(n, 128).T


def _pad(v, n):
    out = np.zeros(v.shape[:-1] + (n,), v.dtype)
    out[..., :v.shape[-1]] = v
    return out


def consts(S):
    f32 = np.float32
    half = 32
    inv = (1.0 / (10000.0 ** (np.arange(half, dtype=f32) / f32(half)))).astype(f32)
    ang = (np.arange(S, dtype=f32)[:, None] * inv[None, :]).astype(f32)
    cos = np.cos(ang).astype(f32).T
    sin = np.sin(ang).astype(f32).T
    cos64 = np.concatenate([cos, cos], 0)
    sin64 = np.concatenate([-sin, sin], 0)
    rope = np.stack([np.concatenate([cos64, cos64], 0), np.concatenate([sin64, sin64], 0),
                     np.concatenate([cos64, np.ones_like(cos64)], 0), np.concatenate([sin64, np.zeros_like(sin64)], 0)], 0)
    p = np.arange(128)[:, None]
    cstb = np.zeros((128, 4480), f32)
    cstb[:, 0:128] = np.eye(128)
    c = np.arange(896)[None, :]
    cstb[:, 128:1024] = np.where((c - 384) - p >= 0, 0.0, -BIG)
    c = np.arange(1408)[None, :]
    u = p - (c - 896)
    cstb[:, 1024:2432] = np.where((u > 0) & (u <= 512), 0.0, -BIG)
    c = np.arange(2048)[None, :]
    cstb[:, 2432:4480] = ((p % 32) == 2 * (c // 128) + (c % 128) // 64).astype(f32)
    cstf = np.zeros((128, 523), f32)
    k = np.arange(128)[:, None]
    l_ = np.arange(128)[None, :]
    cstf[:, 0:128] = (k <= l_)
    cstf[:, 128:256] = 1.0
    cstf[:, 256:384] = (k > l_)
    m = np.arange(8)[None, :]
    cstf[:, 384:392] = (16 * m + 15 <= p)
    lo = (np.arange(128) < 64)
    cstf[:, 392] = np.where(lo, 0.0, 1.0)
    cstf[:, 393] = np.where(lo, 1e9, 0.0)
    cstf[:, 394] = np.where(lo, -1e30, 1e9)
    cstf[:, 395:523] = np.eye(128)
    return rope.astype(f32), cstb, cstf


def prep_shared(inp, NL):
    f32 = np.float32
    cols = _cols()
    d = {}
    d["w_in"] = np.stack([_fm(np.asarray(inp["w_in"][l])[:, cols]) for l in range(NL)])
    d["w_out"] = np.stack([_fm(np.asarray(inp["w_out"][l])) for l in range(NL)])
    d["w_up"] = np.stack([_fm(np.asarray(inp["ffn_w_up"][l])) for l in range(NL)])
    d["w_dn"] = np.ascontiguousarray(np.asarray(inp["ffn_w_down"][:NL]))
    pv = np.zeros((NL, 128, 238), f32)
    rowv = np.zeros((NL, 1, 536), f32)
    w1r = np.zeros((NL, 128, 32, 128), f32)
    posT = np.zeros((NL, 128, 32), f32)
    w2k = np.zeros((NL, 128, 128), f32)
    w2v = np.zeros((NL, 128, 64), f32)
    for l in range(NL):
        pv[l, :, 0:8] = _pc(inp["attn_norm_w"][l], 8)
        pv[l, :, 8:16] = _pc(inp["ffn_norm_w"][l], 8)
        pv[l, :, 16:48] = inp["ssd_conv_w"][l].reshape(4, 8, 128).transpose(2, 1, 0).reshape(128, 32)
        pv[l, :, 48:56] = _pc(inp["ssd_conv_b"][l], 8)
        pv[l, :, 56:62] = inp["sc_conv_w"][l].reshape(3, 2, 128).transpose(2, 1, 0).reshape(128, 6)
        cw = inp["ffn_conv_w"][l]
        cbv = inp["ffn_conv_b"][l]
        for hf in range(2):
            w = _pad(cw[:, hf * D_FF:(hf + 1) * D_FF], 2816)
            pv[l, :, 62 + hf * 66:62 + (hf + 1) * 66] = w.reshape(3, 22, 128).transpose(2, 1, 0).reshape(128, 66)
            pv[l, :, 194 + hf * 22:194 + (hf + 1) * 22] = _pc(_pad(cbv[hf * D_FF:(hf + 1) * D_FF], 2816), 22)
        rowv[l, 0, 0:8] = inp["ssd_dt_bias"][l]
        rowv[l, 0, 8:16] = inp["ssd_a_log"][l]
        rowv[l, 0, 16:24] = inp["ssd_d"][l]
        rowv[l, 0, 24:536] = inp["ssd_norm_w"][l]
        w1r[l, 0:64] = inp["cmp_k_w1"][l].reshape(32, 64, 128).transpose(1, 0, 2)
        w1r[l, 64:128] = inp["cmp_v_w1"][l].reshape(32, 64, 128).transpose(1, 0, 2)
        posT[l, 0:64] = inp["cmp_k_pos"][l].T
        posT[l, 64:128] = inp["cmp_v_pos"][l].T
        w2k[l] = np.concatenate([inp["cmp_k_w2"][l], inp["cmp_k_w2"][l]], 1)
        w2v[l] = inp["cmp_v_w2"][l]
    d.update(pv=pv, rowv=rowv, w1r=w1r, posT=posT, w2k=w2k, w2v=w2v)
    d["fnw"] = np.ascontiguousarray(_pc(np.asarray(inp["final_norm_w"]), 8))
    return d


_NC_CACHE = {}


def run(inp, S, NL, nb, taps=None, n_cores=None):
    inp = {k: np.asarray(v, dtype=np.float32) for k, v in inp.items()}
    key = (S, NL, tuple(sorted(taps)) if taps else None)
    if key not in _NC_CACHE:
        _NC_CACHE[key] = build(S, NL, taps)
    nc = _NC_CACHE[key]
    shared = prep_shared(inp, NL)
    rope, cstb, cstf = consts(S)
    shared.update(rope=rope, cstb=cstb, cstf=cstf)
    n_cores = n_cores or 8
    in_maps = []
    for core in range(n_cores):
        b = core % nb
        x = inp["x"][b, :S]
        m = dict(shared)
        m["xT"] = np.ascontiguousarray(x.T.reshape(8, 128, S).transpose(1, 0, 2))
        in_maps.append(m)
    res = run_bass_kernel_spmd(nc, in_maps, core_ids=list(range(n_cores)))
    outs = []
    for b in range(nb):
        o = res.results[b]["outT"]
        outs.append(o.transpose(2, 1, 0).reshape(S, 1024))
    return np.stack(outs), res


def kernel(**inputs):
    out, _ = run(inputs, 8192, 4, 4)
    return out.astype(np.float32)
```
